# Optimizing a Trainium2 kernel written in Bass

```python
import math
import jax, jax.numpy as jnp
from jax import lax
import numpy as np

D_MODEL = 1024
BATCH = 8
SEQ = 2048
DEPTH = 4
DEC_BATCH = 32
DEC_SEQ = 1
PAST_LEN = 8192
PAGE_SIZE = 128

N_A_LAYERS = DEPTH // 2
N_B_LAYERS = DEPTH - N_A_LAYERS
D_INNER = 2 * D_MODEL
SSM_HEAD_DIM = 64
SSM_HEADS = D_INNER // SSM_HEAD_DIM
SSM_GROUPS = 4
SSM_STATE = 128
CONV_WIDTH = 4
CONV_DIM = D_INNER + 2 * SSM_GROUPS * SSM_STATE
SSD_CHUNK = 128
A_IN_PROJ = D_INNER + CONV_DIM + SSM_HEADS
DIL_GROUPS = ((128, 1), (512, 4), (2048, 16))
N_DIL = len(DIL_GROUPS)
ATT_HEAD_DIM = 64
ATT_HEADS = D_MODEL // ATT_HEAD_DIM
ATT_WIDTH = ATT_HEADS * ATT_HEAD_DIM
Q_WIDTH = N_DIL * ATT_WIDTH
B_IN_PROJ = Q_WIDTH + ATT_WIDTH
KV_WIDTH = 2 * N_DIL * ATT_WIDTH
Q_BLOCK = 128
ATT_SCALE = ATT_HEAD_DIM ** -0.5
LN_EPS = 1e-5
RMS_EPS = 1e-5
DEEPNORM_ALPHA = (2.0 * DEPTH) ** 0.25
DEEPNORM_BETA = (8.0 * DEPTH) ** -0.25

kernel_name = "yoco_ssd_dilated_alibi_deepnorm_step"


def layer_norm(x, g, b):
    xf = x.astype(jnp.float32)
    mu = jnp.mean(xf, -1, keepdims=True)
    var = jnp.mean(jnp.square(xf - mu), -1, keepdims=True)
    return ((xf - mu) * lax.rsqrt(var + LN_EPS) * g.astype(jnp.float32) + b.astype(jnp.float32)).astype(x.dtype)


def gated_group_rmsnorm(y, z, w):
    h = (y * jax.nn.silu(z)).astype(jnp.float32)
    hg = h.reshape(h.shape[:-1] + (SSM_GROUPS, D_INNER // SSM_GROUPS))
    hg = hg * lax.rsqrt(jnp.mean(hg * hg, -1, keepdims=True) + RMS_EPS)
    return (hg.reshape(h.shape) * w.astype(jnp.float32)).astype(y.dtype)


def causal_dwconv(u, prev, w, b):
    ext = jnp.concatenate([prev.astype(u.dtype), u], axis=1)
    out = lax.conv_general_dilated(ext, w.astype(ext.dtype)[:, None, :], window_strides=(1,), padding='VALID',
                                   dimension_numbers=('NWC', 'WIO', 'NWC'), feature_group_count=u.shape[-1])
    return out + b.astype(out.dtype), ext[:, -(CONV_WIDTH - 1):]


def ssd_scan(x, dt, a, bm, cm, h0):
    bsz, seq_len, n_heads, p = x.shape
    g, n = bm.shape[-2:]
    hpg = n_heads // g
    t = min(SSD_CHUNK, seq_len)
    nc = -(-seq_len // t)
    pad = nc * t - seq_len

    def chunk(u):
        u = jnp.pad(u.astype(jnp.float32), [(0, 0), (0, pad)] + [(0, 0)] * (u.ndim - 2))
        return u.reshape((bsz, nc, t) + u.shape[2:])

    x, dt, bm, cm = chunk(x), chunk(dt), chunk(bm), chunk(cm)
    xdt = x * dt[..., None]
    a_cs = jnp.cumsum(dt * a.astype(jnp.float32), axis=2)
    a_cs_h = jnp.moveaxis(a_cs, -1, 2)
    causal = jnp.tril(jnp.ones((t, t), dtype=bool))
    seg = jnp.where(causal, a_cs_h[..., :, None] - a_cs_h[..., None, :], -jnp.inf)
    decay = jnp.exp(seg).reshape(bsz, nc, g, hpg, t, t)
    cb = jnp.einsum('bclgn,bcsgn->bcgls', cm, bm)
    xdt_g = xdt.reshape(bsz, nc, t, g, hpg, p)
    y_diag = jnp.einsum('bcgls,bcgkls,bcsgkp->bclgkp', cb, decay, xdt_g)
    to_end = jnp.exp(a_cs[:, :, -1:, :] - a_cs)
    states = jnp.einsum('bclgn,bclgkp->bcgkpn', bm,
                        (xdt * to_end[..., None]).reshape(bsz, nc, t, g, hpg, p))
    states = states.reshape(bsz, nc, n_heads, p, n)
    chunk_decay = jnp.exp(a_cs[:, :, -1, :])

    def step(h, inp):
        dec, s_c = inp
        return h * dec[..., None, None] + s_c, h

    h_final, h_in = lax.scan(step, h0.astype(jnp.float32),
                             (jnp.moveaxis(chunk_decay, 1, 0), jnp.moveaxis(states, 1, 0)))
    h_in = jnp.moveaxis(h_in, 0, 1).reshape(bsz, nc, g, hpg, p, n)
    y_off = jnp.einsum('bclgn,bcgkpn->bclgkp', cm, h_in) * jnp.exp(a_cs).reshape(bsz, nc, t, g, hpg)[..., None]
    y = (y_diag + y_off).reshape(bsz, nc * t, n_heads, p)[:, :seq_len]
    return y, h_final


def mamba2_mixer(u, conv_prev, ssm_prev, w_in, conv_w, conv_b, dt_bias, a_log, d_skip, norm_w, w_out):
    bsz, seq_len, _ = u.shape
    zxbcdt = u @ w_in
    z, xbc, dt = jnp.split(zxbcdt, [D_INNER, D_INNER + CONV_DIM], axis=-1)
    xbc, conv_new = causal_dwconv(xbc, conv_prev, conv_w, conv_b)
    xbc = jax.nn.silu(xbc)
    xs, bm, cm = jnp.split(xbc, [D_INNER, D_INNER + SSM_GROUPS * SSM_STATE], axis=-1)
    xs = xs.reshape(bsz, seq_len, SSM_HEADS, SSM_HEAD_DIM)
    bm = bm.reshape(bsz, seq_len, SSM_GROUPS, SSM_STATE)
    cm = cm.reshape(bsz, seq_len, SSM_GROUPS, SSM_STATE)
    dt = jax.nn.softplus(dt.astype(jnp.float32) + dt_bias.astype(jnp.float32))
    a = -jnp.exp(a_log.astype(jnp.float32))
    y, ssm_new = ssd_scan(xs, dt, a, bm, cm, ssm_prev)
    y = y + xs.astype(jnp.float32) * d_skip.astype(jnp.float32)[:, None]
    y = gated_group_rmsnorm(y.reshape(bsz, seq_len, D_INNER).astype(u.dtype), z, norm_w)
    return y @ w_out, conv_new, ssm_new


def dilated_group_prompt(q, k, v, window, dil, slopes):
    bsz, seq_len, nh, hd = q.shape
    n = seq_len // dil
    w = window // dil
    qb = min(Q_BLOCK, n)
    nblk = -(-n // qb)
    n_pad = nblk * qb

    def by_residue(u, left):
        u = u.reshape(bsz, n, dil, nh, hd).transpose(0, 2, 1, 3, 4)
        return jnp.pad(u, ((0, 0), (0, 0), (left, n_pad - n), (0, 0), (0, 0)))

    qr = by_residue(q, 0).reshape(bsz, dil, nblk, qb, nh, hd)
    kr = by_residue(k, w)
    vr = by_residue(v, w)
    key_idx = jnp.arange(nblk)[:, None] * qb + jnp.arange(qb + w)[None, :]
    kb = kr[:, :, key_idx]
    vb = vr[:, :, key_idx]
    qi = jnp.arange(nblk)[:, None] * qb + jnp.arange(qb)[None, :]
    kj = key_idx - w
    steps = qi[:, :, None] - kj[:, None, :]
    valid = (steps >= 0) & (steps <= w) & (kj[:, None, :] >= 0)
    dist = (steps * dil).astype(jnp.float32)
    bias = jnp.where(valid[:, None], -slopes[None, :, None, None] * dist[:, None], -jnp.inf)
    scores = jnp.einsum('brcqhd,brckhd->brchqk', qr, kb,
                        preferred_element_type=jnp.float32) * ATT_SCALE + bias
    m = jnp.max(scores, -1)
    pr = jnp.exp(scores - m[..., None])
    s = jnp.sum(pr, -1)
    m = jnp.moveaxis(m, 3, 4)
    s = jnp.moveaxis(s, 3, 4)
    o = jnp.einsum('brchqk,brckhd->brcqhd', pr, vb.astype(jnp.float32)) / s[..., None]

    def back(u):
        u = u.reshape((bsz, dil, n_pad) + u.shape[4:])[:, :, :n]
        u = jnp.swapaxes(u, 1, 2)
        return u.reshape((bsz, seq_len) + u.shape[3:])

    return back(o), back(m), back(s)


def dilated_group_sample(q, k_all, v_all, buf_len, window, dil, slopes):
    n_new = q.shape[1]
    steps = jnp.arange(window // dil + 1)
    idx = buf_len + jnp.arange(n_new)[:, None] - steps[None, :] * dil
    valid = idx >= 0
    idx_c = jnp.maximum(idx, 0)
    kg = k_all[:, idx_c]
    vg = v_all[:, idx_c]
    dist = (steps * dil).astype(jnp.float32)
    bias = jnp.where(valid[:, None, :], -slopes[None, :, None] * dist[None, None, :], -jnp.inf)
    scores = jnp.einsum('bthd,btkhd->bthk', q, kg, preferred_element_type=jnp.float32) * ATT_SCALE + bias
    m = jnp.max(scores, -1)
    pr = jnp.exp(scores - m[..., None])
    s = jnp.sum(pr, -1)
    o = jnp.einsum('bthk,btkhd->bthd', pr, vg.astype(jnp.float32)) / s[..., None]
    return o, m, s


def b_project(h, w_in):
    bsz, seq_len, _ = h.shape
    proj = h @ w_in
    q = proj[..., :Q_WIDTH].reshape(bsz, seq_len, N_DIL, ATT_HEADS, ATT_HEAD_DIM)
    return q, proj[..., Q_WIDTH:]


def b_finish(results, gate, w_out):
    o = jnp.stack([r[0] for r in results])
    m = jnp.stack([r[1] for r in results])
    s = jnp.stack([r[2] for r in results])
    wts = s * jnp.exp(m - jnp.max(m, 0))
    o = jnp.einsum('gblh,gblhd->blhd', wts, o) / jnp.sum(wts, 0)[..., None]
    bsz, seq_len = gate.shape[:2]
    o = o.reshape(bsz, seq_len, ATT_WIDTH).astype(gate.dtype) * jax.nn.silu(gate)
    return o @ w_out


def dilated_mixer_prompt(h, kv_groups, w_in, w_out, slopes):
    q, gate = b_project(h, w_in)
    res = [dilated_group_prompt(q[:, :, g], kv_groups[g][:, :, 0], kv_groups[g][:, :, 1], win, dil, slopes)
           for g, (win, dil) in enumerate(DIL_GROUPS)]
    return b_finish(res, gate, w_out)


def dilated_mixer_sample(h, kv_full, buf_lens, w_in, w_out, slopes):
    q, gate = b_project(h, w_in)
    res = [dilated_group_sample(q[:, :, g], kv_full[g][:, :, 0], kv_full[g][:, :, 1], buf_lens[g], win, dil, slopes)
           for g, (win, dil) in enumerate(DIL_GROUPS)]
    return b_finish(res, gate, w_out)


def setup_inputs(seed: int = 0) -> dict:
    key = jax.random.key(seed)
    ks = jax.random.split(key, 20)
    f32 = jnp.float32

    def nrm(k, shape, scale):
        return jax.random.normal(k, shape, f32) * scale

    buf = [min(w, PAST_LEN) for w, _ in DIL_GROUPS]
    kv_shape = lambda L: (DEC_BATCH, L, 2, ATT_HEADS, ATT_HEAD_DIM)
    dt0 = jnp.exp(jax.random.uniform(ks[10], (N_A_LAYERS, SSM_HEADS), f32, math.log(1e-3), math.log(1e-1)))
    kv_col_scale = jnp.concatenate([jnp.ones((N_DIL * ATT_WIDTH,), f32),
                                    jnp.full((N_DIL * ATT_WIDTH,), DEEPNORM_BETA, f32)])
    return {
        "x_prompt": nrm(ks[0], (BATCH, SEQ, D_MODEL), 1.0),
        "x_sample": nrm(ks[1], (DEC_BATCH, DEC_SEQ, D_MODEL), 1.0),
        "state_ssm": nrm(ks[2], (N_A_LAYERS, DEC_BATCH, SSM_HEADS, SSM_HEAD_DIM, SSM_STATE), 0.5),
        "state_conv": nrm(ks[3], (N_A_LAYERS, DEC_BATCH, CONV_WIDTH - 1, CONV_DIM), 1.0),
        "cache_kv_w128": nrm(ks[4], kv_shape(buf[0]), 1.0),
        "cache_kv_w512": nrm(ks[5], kv_shape(buf[1]), 1.0),
        "cache_kv_w2048": nrm(ks[6], kv_shape(buf[2]), 1.0),
        "a_in_proj": nrm(ks[7], (N_A_LAYERS, D_MODEL, A_IN_PROJ), D_MODEL ** -0.5),
        "a_conv_w": nrm(ks[8], (N_A_LAYERS, CONV_WIDTH, CONV_DIM), CONV_WIDTH ** -0.5),
        "a_conv_b": nrm(ks[9], (N_A_LAYERS, CONV_DIM), 0.02),
        "a_dt_bias": dt0 + jnp.log(-jnp.expm1(-dt0)),
        "a_log": jnp.log(jax.random.uniform(ks[11], (N_A_LAYERS, SSM_HEADS), f32, 1.0, 16.0)),
        "a_d": 1.0 + nrm(ks[12], (N_A_LAYERS, SSM_HEADS), 0.02),
        "a_norm_w": 1.0 + nrm(ks[13], (N_A_LAYERS, D_INNER), 0.02),
        "a_out_proj": nrm(ks[14], (N_A_LAYERS, D_INNER, D_MODEL), D_INNER ** -0.5 * DEEPNORM_BETA),
        "kv_proj": nrm(ks[15], (D_MODEL, KV_WIDTH), D_MODEL ** -0.5) * kv_col_scale,
        "b_in_proj": nrm(ks[16], (N_B_LAYERS, D_MODEL, B_IN_PROJ), D_MODEL ** -0.5),
        "b_out_proj": nrm(ks[17], (N_B_LAYERS, ATT_WIDTH, D_MODEL), ATT_WIDTH ** -0.5 * DEEPNORM_BETA),
        "ln_g": 1.0 + nrm(ks[18], (DEPTH, D_MODEL), 0.02),
        "ln_b": nrm(ks[19], (DEPTH, D_MODEL), 0.02),
    }


def reference(x_prompt, x_sample, state_ssm, state_conv, cache_kv_w128, cache_kv_w512, cache_kv_w2048,
              a_in_proj, a_conv_w, a_conv_b, a_dt_bias, a_log, a_d, a_norm_w, a_out_proj,
              kv_proj, b_in_proj, b_out_proj, ln_g, ln_b):
    bsz_p, seq_p, _ = x_prompt.shape
    bsz_s, seq_s, _ = x_sample.shape
    slopes = jnp.exp2(-8.0 * jnp.arange(1, ATT_HEADS + 1, dtype=jnp.float32) / ATT_HEADS)
    caches = (cache_kv_w128, cache_kv_w512, cache_kv_w2048)
    buf_lens = [c.shape[1] for c in caches]
    hp, hs = x_prompt, x_sample
    ssm_p_list, conv_p_list, ssm_s_list, conv_s_list = [], [], [], []
    kv_groups_p, kv_full_s, new_kv_p, new_kv_s = [], [], [], []
    for layer in range(DEPTH):
        if layer < N_A_LAYERS:
            i = layer
            params = (a_in_proj[i], a_conv_w[i], a_conv_b[i], a_dt_bias[i], a_log[i], a_d[i], a_norm_w[i], a_out_proj[i])
            conv0 = jnp.zeros((bsz_p, CONV_WIDTH - 1, CONV_DIM), hp.dtype)
            ssm0 = jnp.zeros((bsz_p, SSM_HEADS, SSM_HEAD_DIM, SSM_STATE), jnp.float32)
            dp, conv_p, ssm_p = mamba2_mixer(hp, conv0, ssm0, *params)
            ds, conv_s, ssm_s = mamba2_mixer(hs, state_conv[i], state_ssm[i], *params)
            conv_p_list.append(conv_p)
            ssm_p_list.append(ssm_p)
            conv_s_list.append(conv_s.astype(state_conv.dtype))
            ssm_s_list.append(ssm_s.astype(state_ssm.dtype))
        else:
            if layer == N_A_LAYERS:
                kv_p = (hp @ kv_proj).reshape(bsz_p, seq_p, 2, N_DIL, ATT_HEADS, ATT_HEAD_DIM)
                kv_s = (hs @ kv_proj).reshape(bsz_s, seq_s, 2, N_DIL, ATT_HEADS, ATT_HEAD_DIM)
                for g, ((win, dil), cache) in enumerate(zip(DIL_GROUPS, caches)):
                    rows_p = kv_p[:, :, :, g]
                    kv_groups_p.append(rows_p)
                    new_kv_p.append(rows_p[:, -min(win, seq_p):])
                    rows_s = kv_s[:, :, :, g]
                    full = jnp.concatenate([cache.astype(rows_s.dtype), rows_s], axis=1)
                    kv_full_s.append(full)
                    new_kv_s.append(full[:, -min(win, full.shape[1]):].astype(cache.dtype))
            j = layer - N_A_LAYERS
            dp = dilated_mixer_prompt(hp, kv_groups_p, b_in_proj[j], b_out_proj[j], slopes)
            ds = dilated_mixer_sample(hs, kv_full_s, buf_lens, b_in_proj[j], b_out_proj[j], slopes)
        hp = layer_norm(DEEPNORM_ALPHA * hp + dp, ln_g[layer], ln_b[layer])
        hs = layer_norm(DEEPNORM_ALPHA * hs + ds, ln_g[layer], ln_b[layer])
    ssm_prompt = jnp.stack(ssm_p_list)
    conv_prompt = jnp.stack(conv_p_list)
    ssm_sample = jnp.stack(ssm_s_list)
    conv_sample = jnp.stack(conv_s_list)
    kv_w128_prompt, kv_w512_prompt, kv_w2048_prompt = new_kv_p
    kv_w128_sample, kv_w512_sample, kv_w2048_sample = new_kv_s
    return (hp, hs, ssm_prompt, conv_prompt, kv_w128_prompt, kv_w512_prompt, kv_w2048_prompt,
            ssm_sample, conv_sample, kv_w128_sample, kv_w512_sample, kv_w2048_sample)
```

```python
import numpy as np
from contextlib import ExitStack
import concourse.bass as bass
import concourse.mybir as mybir
from concourse.bass_utils import run_bass_kernel_spmd

F32 = mybir.dt.float32
BF16 = mybir.dt.bfloat16
AF = mybir.ActivationFunctionType
ALU = mybir.AluOpType
AX = mybir.AxisListType

SEM_LIMIT = 30000
NCORES = 8
SEQ = 2048
D = 1024
DI = 2048
NH = 32
CD = 3072
AIN = 5152
NSAMP = 4
NCHP = 16
NCH = 20
DEPTH = 4
ALPHA = (2.0 * DEPTH) ** 0.25
LN_EPS = 1e-5
RMS_EPS = 1e-5
DILS = (1, 4, 16)
WINS = (128, 512, 2048)
ACCW = 1056


class Buf:
    __slots__ = ("name", "w", "r", "sem", "dcount", "excl", "sem_sw", "dcount_sw")

    def __init__(self, name):
        self.name = name
        self.excl = False
        self.sem_sw = None
        self.dcount_sw = 0
        self.w = {}
        self.r = {}
        self.sem = None
        self.dcount = 0


class Q:
    def __init__(self, k, name, self_sync):
        self.k = k
        self.name = name
        self.self_sync = self_sync
        self.ops = []
        self.sem = None
        self.count = 0
        self.seen = {}
        self.sems_used = []

    def _newsem(self):
        self.sem = self.k.new_sem(f"q_{self.name}_{len(self.k.sems)}")
        self.sems_used.append(self.sem)
        self.count = 0


class K:
    def __init__(self, nc, stack):
        self.nc = nc
        self.stack = stack
        self.sems = []
        self.pe = Q(self, "pe", False)
        self.act = Q(self, "act", True)
        self.dve = Q(self, "dve", True)
        self.pool = Q(self, "pool", True)
        self.sp = Q(self, "sp", True)
        self.queues = [self.pe, self.act, self.dve, self.pool, self.sp]
        self.bufs = []
        self.named = {}
        self.final = {}
        self.nobar = set()

    def new_sem(self, name):
        s = self.stack.enter_context(self.nc.semaphore(name))
        self.sems.append(s)
        return s

    def buf(self, name="b"):
        if name not in self.named:
            self.named[name] = Buf(name)
            self.bufs.append(self.named[name])
        return self.named[name]

    def sb(self, name, shape, dtype):
        return self.stack.enter_context(self.nc.sbuf_tensor(name, list(shape), dtype))

    def ps(self, name, shape, dtype=F32):
        return self.stack.enter_context(self.nc.psum_tensor(name, list(shape), dtype))

    def _deps(self, q, reads, writes):
        deps = {}
        for b in reads:
            for d in ((b.w, b.r) if b.excl else (b.w,)):
                for sid, (s, v) in d.items():
                    if deps.get(sid, (None, -1))[1] < v:
                        deps[sid] = (s, v)
        for b in writes:
            for d in (b.w, b.r):
                for sid, (s, v) in d.items():
                    if deps.get(sid, (None, -1))[1] < v:
                        deps[sid] = (s, v)
        waits = []
        for sid, (s, v) in deps.items():
            if (not q.self_sync) and q.sem is not None and sid == id(q.sem):
                continue
            if q.seen.get(sid, -1) >= v:
                continue
            q.seen[sid] = v
            waits.append((s, v))
        return waits

    @staticmethod
    def _mark(bufs_r, bufs_w, sem, val):
        sid = id(sem)
        for b in bufs_r:
            if b.r.get(sid, (None, -1))[1] < val:
                b.r[sid] = (sem, val)
        for b in bufs_w:
            if b.w.get(sid, (None, -1))[1] < val:
                b.w[sid] = (sem, val)

    def op(self, q, fn, reads=(), writes=()):
        if q.sem is None or q.count >= SEM_LIMIT:
            q._newsem()
        waits = self._deps(q, reads, writes)
        q.count += 1
        sem, val = q.sem, q.count
        self.final[id(sem)] = (sem, val)

        def emit(e, fn=fn, waits=waits, sem=sem):
            for (s, v) in waits:
                e.wait_ge(s, v)
            fn(e).then_inc(sem, 1)
        q.ops.append(emit)
        self._mark(reads, writes, sem, val)

    def dma(self, q, out, in_, reads, writes, owner, **kw):
        if q is self.pool:
            if owner.sem_sw is None:
                owner.sem_sw = self.new_sem(f"dsw_{owner.name}")
            owner.dcount_sw += 16
            sem, val = owner.sem_sw, owner.dcount_sw
        else:
            if owner.sem is None:
                owner.sem = self.new_sem(f"d_{owner.name}")
            owner.dcount += 16
            sem, val = owner.sem, owner.dcount
        waits = self._deps(q, reads, writes)
        self.final[id(sem)] = (sem, val)

        def emit(e, waits=waits, sem=sem, out=out, in_=in_, kw=kw):
            for (s, v) in waits:
                e.wait_ge(s, v)
            e.dma_start(out=out, in_=in_, **kw).then_inc(sem, 16)
        q.ops.append(emit)
        self._mark(reads, writes, sem, val)

    def barrier(self, qs=None, final=False):
        for q in (qs or self.queues):
            waits = []
            for sid, (s, v) in self.final.items():
                if (not final) and sid in self.nobar:
                    continue
                if q.seen.get(sid, -1) >= v:
                    continue
                if (not q.self_sync) and q.sem is not None and sid == id(q.sem):
                    continue
                q.seen[sid] = v
                waits.append((s, v))

            def emit(e, waits=waits):
                for (s, v) in waits:
                    e.wait_ge(s, v)
            q.ops.append(emit)

    def emit_all(self):
        nc = self.nc
        with nc.Block() as block:
            @block.tensor
            def _(e):
                for f in self.pe.ops:
                    f(e)

            @block.scalar
            def _(e):
                for f in self.act.ops:
                    f(e)

            @block.vector
            def _(e):
                for f in self.dve.ops:
                    f(e)

            @block.gpsimd
            def _(e):
                for f in self.pool.ops:
                    f(e)

            @block.sync
            def _(e):
                for f in self.sp.ops:
                    f(e)

    def mm(self, out, lhsT, rhs, start, stop, R, W):
        self.op(self.pe, lambda e: e.matmul(out, lhsT=lhsT, rhs=rhs, start=start, stop=stop), R, W)

    def tr(self, out, in_, ident, R, W):
        self.op(self.pe, lambda e: e.transpose(out, in_, ident), R, W)

    def actf(self, out, in_, func, R, W, bias=None, scale=None, accum=None):
        kw = {}
        if bias is not None:
            kw["bias"] = bias
        if scale is not None:
            kw["scale"] = scale
        if accum is not None:
            kw["accum_out"] = accum
        self.op(self.act, lambda e: e.activation(out=out, in_=in_, func=func, **kw), R, W)

    def tt(self, q, out, in0, in1, op, R, W):
        self.op(q, lambda e: e.tensor_tensor(out=out, in0=in0, in1=in1, op=op), R, W)

    def ts(self, q, out, in0, s1, s2, op0, op1, R, W):
        if s2 is None:
            self.op(q, lambda e: e.tensor_scalar(out=out, in0=in0, scalar1=s1, scalar2=None, op0=op0), R, W)
        else:
            self.op(q, lambda e: e.tensor_scalar(out=out, in0=in0, scalar1=s1, scalar2=s2, op0=op0, op1=op1), R, W)

    def stt(self, q, out, in0, scalar, in1, op0, op1, R, W):
        self.op(q, lambda e: e.scalar_tensor_tensor(out=out, in0=in0, scalar=scalar, in1=in1, op0=op0, op1=op1), R, W)

    def cp(self, q, out, in_, R, W):
        if q is self.act:
            self.op(q, lambda e: e.activation(out=out, in_=in_, func=AF.Copy), R, W)
        else:
            self.op(q, lambda e: e.tensor_copy(out=out, in_=in_), R, W)

    def ms(self, q, ap, val, W):
        self.op(q, lambda e: e.memset(ap, val), [], W)

    def red(self, out, in_, op, R, W):
        self.op(self.dve, lambda e: e.tensor_reduce(out=out, in_=in_, axis=AX.X, op=op), R, W)

    def recip(self, out, in_, R, W):
        self.op(self.dve, lambda e: e.reciprocal(out=out, in_=in_), R, W)


class Arena:
    def __init__(self, k, nbytes):
        self.n4 = nbytes // 4
        self.t = k.sb("arena", [128, self.n4], F32)
        self.off = 0

    def reset(self):
        self.off = 0

    def alloc(self, free_shape, dtype, parts=128):
        n = int(np.prod(free_shape))
        esz = 4 if dtype == F32 else 2
        nb4 = (n * esz + 3) // 4
        nb4 = (nb4 + 7) // 8 * 8
        assert self.off + nb4 <= self.n4, f"arena overflow {self.off}+{nb4}>{self.n4}"
        ap = self.t[0:parts, self.off:self.off + nb4]
        self.off += nb4
        if dtype != F32:
            ap = ap.bitcast(dtype)
        ap = ap[:, 0:n]
        if len(free_shape) == 2:
            ap = ap.rearrange("p (a b) -> p a b", a=free_shape[0])
        elif len(free_shape) == 3:
            ap = ap.rearrange("p (a b c) -> p a b c", a=free_shape[0], b=free_shape[1])
        return ap


class Prog:
    def __init__(self, phases=("A0", "A1", "KV", "B0", "B1", "CACHE"), debug=False):
        self.phases = phases
        self.debug = debug
        self.nc = nc = bass.Bass("TRN2", target_bir_lowering=False)
        self.stack = ExitStack()
        self.k = K(nc, self.stack)

        def din(name, shape):
            return nc.dram_tensor(name, list(shape), F32, kind="ExternalInput").ap()

        def dout(name, shape):
            return nc.dram_tensor(name, list(shape), F32, kind="ExternalOutput").ap()

        def dscr(name, shape, dt):
            kind = "ExternalOutput" if debug else "Internal"
            return nc.dram_tensor(name, list(shape), dt, kind=kind).ap()

        self.xp = din("xp", [SEQ, D])
        self.xs = din("xs", [NSAMP, D])
        self.st_ssm = din("st_ssm", [2, NSAMP, DI, 128])
        self.st_conv = din("st_conv", [2, NSAMP, 3, CD])
        self.ck = [din("ck128", [NSAMP, 128, 2048]), din("ck512", [NSAMP, 512, 2048]),
                   din("ck2048", [NSAMP, 2048, 2048])]
        self.a_in = din("a_in", [2, D, AIN])
        self.a_cw = din("a_cw", [2, 4, CD])
        self.a_cb = din("a_cb", [2, CD])
        self.a_dtb = din("a_dtb", [2, NH])
        self.a_log = din("a_log", [2, NH])
        self.a_d = din("a_d", [2, NH])
        self.a_nw = din("a_nw", [2, DI])
        self.a_out = din("a_out", [2, DI, D])
        self.kvw = din("kvw", [D, 6144])
        self.b_in = din("b_in", [2, D, 4096])
        self.b_out = din("b_out", [2, D, D])
        self.ln_g = din("ln_g", [4, D])
        self.ln_b = din("ln_b", [4, D])
        self.yp = dout("yp", [SEQ, D])
        self.ys = dout("ys", [NSAMP, D])
        self.ssm_p = dout("ssm_p", [2, DI, 128])
        self.conv_p = dout("conv_p", [2, 3, CD])
        self.kvp = [dout("kv128_p", [128, 2048]), dout("kv512_p", [512, 2048]), dout("kv2048_p", [2048, 2048])]
        self.ssm_s = dout("ssm_s", [2, NSAMP, DI, 128])
        self.conv_s = dout("conv_s", [2, NSAMP, 3, CD])
        self.kvs = [dout("kv128_s", [NSAMP, 128, 2048]), dout("kv512_s", [NSAMP, 512, 2048]),
                    dout("kv2048_s", [NSAMP, 2048, 2048])]
        self.S = dscr("S", [NCH * 128, D], F32)
        self.G = dscr("G", [NCH * 128, DI], BF16)
        self.KTs = dscr("KTs", [3, D, SEQ], BF16)
        self.Vs = dscr("Vs", [3, SEQ, D], BF16)
        self.ACC = dscr("ACC", [3, SEQ + 128, ACCW], F32)
        self.KVS = dscr("KVS", [NSAMP, 6144], F32)

        k = self.k
        self.bext = k.buf("ext")
        self.bS = [k.buf(f"S{i}") for i in range(NCH)]
        self.bG = [k.buf(f"G{i}") for i in range(NCH)]
        self.consts()
        self.arena = Arena(k, 197 * 1024)
        self.build()
        k.barrier(final=True)
        k.emit_all()
        self.stack.close()

    def consts(self):
        k = self.k
        self.bconst = bc = k.buf("const")
        self.idf = k.sb("idf", [128, 128], F32)
        self.idb = k.sb("idb", [128, 128], BF16)
        self.utri = k.sb("utri", [128, 128], BF16)
        self.ltri = k.sb("ltri", [128, 128], BF16)
        self.causT = k.sb("causT", [128, 128], F32)
        self.onesb = k.sb("onesb", [128, 128], BF16)
        self.onehot0 = k.sb("onehot0", [128, 1], F32)
        tmpf = k.sb("tmpf", [128, 128], F32)
        P = k.pool
        k.ms(P, self.idf[:, :], 0.0, [bc])
        k.op(P, lambda e: e.affine_select(out=self.idf[:, :], in_=self.idf[:, :], pattern=[[-1, 128]],
                                          compare_op=ALU.not_equal, fill=1.0, base=0, channel_multiplier=1), [bc], [bc])
        k.cp(P, self.idb[:, :], self.idf[:, :], [bc], [bc])
        k.ms(P, tmpf[:, :], 1.0, [bc])
        k.op(P, lambda e: e.affine_select(out=tmpf[:, :], in_=tmpf[:, :], pattern=[[1, 128]],
                                          compare_op=ALU.is_ge, fill=0.0, base=0, channel_multiplier=-1), [bc], [bc])
        k.cp(P, self.utri[:, :], tmpf[:, :], [bc], [bc])
        k.cp(P, self.causT[:, :], tmpf[:, :], [bc], [bc])
        k.ms(P, tmpf[:, :], 1.0, [bc])
        k.op(P, lambda e: e.affine_select(out=tmpf[:, :], in_=tmpf[:, :], pattern=[[-1, 128]],
                                          compare_op=ALU.is_gt, fill=0.0, base=0, channel_multiplier=1), [bc], [bc])
        k.cp(P, self.ltri[:, :], tmpf[:, :], [bc], [bc])
        k.ms(P, tmpf[:, :], 1.0, [bc])
        k.cp(P, self.onesb[:, :], tmpf[:, :], [bc], [bc])
        k.cp(P, self.onehot0[:, :], self.idf[:, 0:1], [bc], [bc])
        self.PY = k.ps("PY", [128, 2048])
        self.PA = k.ps("PA", [128, 1024])
        self.PB = k.ps("PB", [128, 512])
        self.PC = k.ps("PC", [128, 512])
        self.bPY = [k.buf(f"PY{i}") for i in range(4)]
        self.bPA = [k.buf(f"PA{i}") for i in range(2)]
        self.bPB = [k.buf("PB0")] * 2
        self.bPC = [k.buf("PC0")] * 2
        for b in self.bPY + self.bPA + self.bPB + self.bPC:
            b.excl = True
        self.hTall = None
        self.bhT = [k.buf(f"hT{i}") for i in range(NCHP)]
        self.hsT = k.sb("hsT", [128, 8, NSAMP], BF16)
        self.bhsT = k.buf("hsT")
        self.knewT = k.sb("knewT", [128, 24, NSAMP], BF16)
        self.bknew = k.buf("knewT")
        self.bKT = [k.buf(f"KTs{g}") for g in range(3)]
        self.bVs = [k.buf(f"Vs{g}") for g in range(3)]
        self.bACC = [k.buf(f"ACC{g}") for g in range(3)]
        self.bKVS = k.buf("KVS")
        self.lng = k.sb("lng", [128, D], F32)
        self.lnb = k.sb("lnb", [128, D], F32)
        self.bln = k.buf("ln")

    def load_ln(self, layer):
        k = self.k
        k.dma(k.sp, self.lng[:, :], self.ln_g[layer:layer + 1, :].partition_broadcast(128), [self.bext], [self.bln], self.bln)
        k.dma(k.sp, self.lnb[:, :], self.ln_b[layer:layer + 1, :].partition_broadcast(128), [self.bext], [self.bln], self.bln)

    def resid_ln(self, q_ps, ps_bufs, hin, bhin, r, br, stat, bstat, junk, bjunk, np_=128):
        k = self.k
        V = k.dve
        k.stt(V, r, hin, ALPHA, q_ps, ALU.mult, ALU.add, [bhin] + ps_bufs, [br])
        k.red(stat[:, 0:1], r, ALU.add, [br], [bstat])
        k.ms(k.pool, stat[:, 1:2], 0.0, [bstat])
        k.actf(junk, r, AF.Square, [br, bstat], [bjunk, bstat], accum=stat[:, 1:2])
        k.ts(V, stat[:, 2:3], stat[:, 0:1], 1.0 / D, None, ALU.mult, None, [bstat], [bstat])
        k.tt(V, stat[:, 3:4], stat[:, 2:3], stat[:, 2:3], ALU.mult, [bstat], [bstat])
        k.stt(V, stat[:, 4:5], stat[:, 1:2], 1.0 / D, stat[:, 3:4], ALU.mult, ALU.subtract, [bstat], [bstat])
        k.ts(V, stat[:, 4:5], stat[:, 4:5], LN_EPS, None, ALU.add, None, [bstat], [bstat])
        k.actf(stat[:, 5:6], stat[:, 4:5], AF.Ln, [bstat], [bstat])
        k.actf(stat[:, 6:7], stat[:, 5:6], AF.Exp, [bstat], [bstat], scale=-0.5)
        k.ts(V, r, r, stat[:, 2:3], stat[:, 6:7], ALU.subtract, ALU.mult, [br, bstat], [br])
        k.tt(V, r, r, self.lng[0:np_, :], ALU.mult, [br, self.bln], [br])
        k.tt(V, r, r, self.lnb[0:np_, :], ALU.add, [br, self.bln], [br])

    def build(self):
        for L in range(2):
            if f"A{L}" in self.phases:
                self.a_sweep1(L)
                self.k.barrier()
                self.a_sweep2(L)
                self.k.barrier()
        if "KV" in self.phases:
            self.kv_phase()
            self.k.barrier()
        for j in range(2):
            if f"B{j}" in self.phases:
                self.b_phase(j)
                self.k.barrier()

    def tok_ap(self, g, kc, pos0, cnt):
        dil = DILS[g]
        n = SEQ // dil
        hv = self.hTall[:, kc, :]
        if dil == 1:
            return hv[:, pos0:pos0 + cnt]
        hv = hv.rearrange("p (i r) -> p r i", r=dil)
        r0, i0 = pos0 // n, pos0 % n
        if cnt <= n - i0:
            return hv[:, r0, i0:i0 + cnt]
        assert i0 == 0 and cnt % n == 0
        return hv[:, r0:r0 + cnt // n, :]

    def row_ap(self, dram2d, g, pos0):
        dil = DILS[g]
        n = SEQ // dil
        r0, i0 = pos0 // n, pos0 % n
        if dil == 1:
            return dram2d[pos0:pos0 + 128, :]
        return dram2d.rearrange("(i r) c -> r i c", r=dil)[r0, i0:i0 + 128, :]

    def cache_copy(self):
        k = self.k
        bcc = k.buf("cachecopy")
        for g in range(3):
            W = WINS[g]
            for b in range(NSAMP):
                nsplit = max(1, W // 512)
                rows = (W - 1)
                step = (rows + nsplit - 1) // nsplit
                for r0 in range(0, rows, step):
                    r1 = min(rows, r0 + step)
                    k.dma(k.act, self.kvs[g][b, r0:r1, :], self.ck[g][b, r0 + 1:r1 + 1, :], [self.bext], [self.bext], bcc)
        k.nobar.add(id(bcc.sem))

    def a_sweep1(self, L):
        k = self.k
        A = self.arena
        A.reset()
        PE, ACT, V, PL, SP = k.pe, k.act, k.dve, k.pool, k.sp
        bext, bc = self.bext, self.bconst
        PY, PA, PB, PC = self.PY, self.PA, self.PB, self.PC
        bPY, bPA, bPB, bPC = self.bPY, self.bPA, self.bPB, self.bPC
        PAb = PA[:, :].bitcast(BF16)
        PYb = PY[:, :].bitcast(BF16)
        PCb = PC[:, :].bitcast(BF16)
        idf, idb = self.idf, self.idb

        Win = A.alloc((8, AIN), BF16); bWin = k.buf("Win")
        dg = [A.alloc((4, 128), BF16) for _ in range(2)]; bdg = [k.buf(f"dg{i}") for i in range(2)]
        diagD = A.alloc((16, 128), BF16); bdiagD = k.buf("diagD")
        cw = A.alloc((24, 4), F32); cbias = A.alloc((24,), F32); dcol = A.alloc((16,), F32); bcw = k.buf("cw")
        dtb = A.alloc((NH,), F32); abc = A.alloc((NH,), F32); bsm = k.buf("smallw")
        hin = [A.alloc((D,), F32) for _ in range(2)]; bhin = [k.buf(f"hin{i}") for i in range(2)]
        hT = A.alloc((8, 256), BF16); bhT = [k.buf(f"hTl{i}") for i in range(2)]
        xpre = A.alloc((24, 259), BF16); bxpre = k.buf("xpre")
        xbcT = A.alloc((24, 256), BF16); bxbc = k.buf("xbcT")
        sz = A.alloc((DI,), F32); bsz = k.buf("sz"); bszg = [k.buf(f"szg{i}") for i in range(4)]
        xdt = A.alloc((DI,), BF16); bxdt = k.buf("xdt")
        xdts = A.alloc((DI,), BF16); bxdts = k.buf("xdts")
        Btok = A.alloc((512,), BF16); bBtok = k.buf("Btok")
        Rt = [A.alloc((8, 128), BF16) for _ in range(2)]; bR = [k.buf(f"R{i}") for i in range(2)]
        E = [A.alloc((8, 128), BF16) for _ in range(2)]; bE = [k.buf(f"E{i}") for i in range(2)]
        Mg = A.alloc((NH, 128), BF16); bMg = [k.buf(f"Mg{i}") for i in range(4)]
        cbm = [A.alloc((128,), BF16) for _ in range(2)]; bcbm = [k.buf(f"cbm{i}") for i in range(2)]
        yo = A.alloc((512,), BF16); byo = k.buf("yo")
        gn = A.alloc((DI,), BF16); bgn = k.buf("gn")
        junk = yo; bjunk = byo
        hs = A.alloc((DI,), F32); bhs = [k.buf(f"hs{i}") for i in range(4)]
        hb = A.alloc((DI,), BF16); bhb = [k.buf(f"hb{i}") for i in range(4)]
        hso = sz.rearrange("p (a b) -> p a b", a=16); bhso = bsz
        sm = A.alloc((512,), F32); bsmt = k.buf("sm")
        dAhl = A.alloc((2, NH), BF16); bdA = k.buf("dAhl")
        last3 = A.alloc((24, 3), F32); blast3 = k.buf("last3")
        newrow = A.alloc((24, 2), F32); bnewrow = k.buf("newrow")
        ss = A.alloc((16,), F32); bss = k.buf("ss")
        craw = A.alloc((3, 128), F32, parts=24); bcraw = k.buf("craw")
        rowst = A.alloc((640,), F32, parts=24); browst = k.buf("rowst")
        dt_raw, dt_abs, dt_, dA, acs, alast, ee, cd, te, dts = [sm[:, i * 32:(i + 1) * 32] for i in range(10)]

        if not (L == 1 and getattr(self, "win_prefetched", False)):
            for kc in range(8):
                k.dma(PL, Win[:, kc, :], self.a_in[L, kc * 128:(kc + 1) * 128, :], [bext], [bWin], bWin)
        cwraw = A.alloc((4, 128), F32, parts=24); cbraw = A.alloc((128,), F32, parts=24); braw = k.buf("raw")
        k.dma(SP, cwraw, self.a_cw[L].rearrange("t (c p) -> c t p", p=128), [bext], [braw], braw)
        k.dma(SP, cbraw, self.a_cb[L].rearrange("(c p) -> c p", p=128), [bext], [braw], braw)
        for t in range(4):
            k.tr(PB[:, t * 24:(t + 1) * 24], cwraw[:, t, :], idf[0:24, 0:24], [braw, bc], [bPB[0]])
        k.cp(V, cw, PB[:, 0:96].rearrange("p (t c) -> p c t", t=4), [bPB[0]], [bcw])
        k.tr(PB[:, 256:280], cbraw, idf[0:24, 0:24], [braw, bc], [bPB[1]])
        k.cp(V, cbias, PB[:, 256:280], [bPB[1]], [bcw])
        dfull = A.alloc((NH,), F32)
        k.dma(SP, dfull, self.a_d[L:L + 1, :].partition_broadcast(128), [bext], [bsm], bsm)
        for two in range(2):
            k.cp(V, dcol[two * 64:(two + 1) * 64, :], dfull[two * 64:(two + 1) * 64, :].rearrange("p (c two) -> p two c", two=2)[:, two, :],
                 [bsm], [bcw])
        k.dma(SP, dtb, self.a_dtb[L:L + 1, :].partition_broadcast(128), [bext], [bsm], bsm)
        k.dma(SP, abc, self.a_log[L:L + 1, :].partition_broadcast(128), [bext], [bsm], bsm)
        k.actf(abc, abc, AF.Exp, [bsm], [bsm])
        k.ts(V, abc, abc, -1.0, None, ALU.mult, None, [bsm], [bsm])
        for fc in range(16):
            k.ts(PL, diagD[:, fc, :], idb[:, :], dcol[:, fc:fc + 1], None, ALU.mult, None, [bc, bcw], [bdiagD])
        k.ms(PL, ss, 0.0, [bss])

        def state_out(dst):
            for half in range(2):
                for bl in range(8):
                    blk = half * 8 + bl
                    k.tr(PA[:, bl * 128:(bl + 1) * 128], hs[:, blk * 128:(blk + 1) * 128], idf[:, :],
                         [bhs[blk // 4], bc], [bPA[bl // 4]])
                k.cp(V if half else ACT, hso[:, half * 8:(half + 1) * 8, :],
                     PA[:, :].rearrange("p (a b) -> p a b", a=8), bPA, [bhso] + bszg)
            k.dma(PL, dst.rearrange("(a p) n -> p a n", p=128), hso, [bhso] + bszg, [bext], bhso)

        nsc = NCH // 2

        def load_hin(c, j):
            if L == 0:
                if c < NCHP:
                    k.dma(SP, hin[j], self.xp[c * 128:(c + 1) * 128, :], [bext], [bhin[j]], bhin[j])
                else:
                    b = c - NCHP
                    k.ms(PL, hin[j], 0.0, [bhin[j]])
                    k.dma(SP, hin[j][0:1, :], self.xs[b:b + 1, :], [bext], [bhin[j]], bhin[j])
                    k.dma(PL, self.S[c * 128:(c + 1) * 128, :], hin[j], [bhin[j]], [self.bS[c]], bhin[j])
            else:
                k.dma(SP, hin[j], self.S[c * 128:(c + 1) * 128, :], [self.bS[c]], [bhin[j]], bhin[j])

        for j in range(2):
            load_hin(j, j)
        for sc in range(nsc):
            samp = sc >= NCHP // 2
            if sc == 1 and L == 1 and "CACHE" in self.phases:
                self.cache_copy()
            for j in range(2):
                c = 2 * sc + j
                for kc in range(8):
                    k.tr(PA[:, kc * 128:(kc + 1) * 128], hin[j][:, kc * 128:(kc + 1) * 128], idf[:, :],
                         [bhin[j], bc], [bPA[kc // 4]])
                k.cp(ACT, hT[:, :, j * 128:(j + 1) * 128], PA[:, :].rearrange("p (a b) -> p a b", a=8), bPA, [bhT[j]])
                if sc + 1 < nsc:
                    load_hin(2 * (sc + 1) + j, j)
            if sc == 0 or samp:
                k.ms(PL, xpre[:, :, 0:3], 0.0, [bxpre])
            else:
                k.cp(V, xpre[:, :, 0:3], xpre[:, :, 256:259], [bxpre], [bxpre])
            for fc in range(24):
                ps, bps = (PB[:, 0:256], bPB[0]) if fc % 2 == 0 else (PC[:, 0:256], bPC[0])
                for kc in range(8):
                    k.mm(ps, Win[:, kc, 2048 + fc * 128:2048 + (fc + 1) * 128], hT[:, kc, :], kc == 0, kc == 7,
                         [bWin] + bhT, [bps])
                k.cp(V if fc % 2 == 0 else ACT, xpre[:, fc, 3:259], ps, [bps], [bxpre])
                if sc == NCHP // 2 - 1:
                    k.cp(V, last3[:, fc, :], ps[:, 253:256], [bps], [blast3])
                if samp:
                    k.cp(V, newrow[:, fc, :], ps.rearrange("p (j t) -> p j t", j=2)[:, :, 0], [bps], [bnewrow])
            if sc == NCHP // 2 - 1:
                for rr in range(3):
                    k.tr(PC[0:24, 128 * rr:128 * (rr + 1)], last3[:, :, rr], idf[:, :], [blast3, bc], [bPC[0]])
                k.cp(V, rowst[:, 0:384], PC[0:24, 0:384], [bPC[0]], [browst])
                k.dma(PL, self.conv_p[L].rearrange("r (c p) -> c r p", p=128), rowst[:, 0:384].rearrange("c (r p) -> c r p", r=3),
                      [browst], [bext], browst)
            if samp:
                for j in range(2):
                    b = 2 * sc + j - NCHP
                    k.dma(SP, craw, self.st_conv[L, b].rearrange("r (c p) -> c r p", p=128), [bext], [bcraw], bcraw)
                    for rr in range(3):
                        k.tr(PC[:, rr * 24:(rr + 1) * 24], craw[:, rr, :], idf[0:24, 0:24], [bcraw, bc], [bPC[0]])
                    k.cp(V, xpre[:, :, j * 128:j * 128 + 3], PC[:, 0:72].rearrange("p (r c) -> p c r", r=3), [bPC[0]], [bxpre])
                    k.tr(PC[0:24, 128:256], newrow[:, :, j], idf[:, :], [bnewrow, bc], [bPC[0]])
                    k.cp(V, rowst[:, 384 + 128 * j:512 + 128 * j], PC[0:24, 128:256], [bPC[0]], [browst])
                    k.dma(PL, self.conv_s[L, b, 2, :].rearrange("(c p) -> c p", p=128), rowst[:, 384 + 128 * j:512 + 128 * j],
                          [browst], [bext], browst)
                    k.dma(SP, self.conv_s[L, b, 0:2, :], self.st_conv[L, b, 1:3, :], [bext], [bext], browst)
            for fc in range(24):
                ps, bps = (PB[:, 0:256], bPB[0]) if fc % 2 == 0 else (PC[:, 0:256], bPC[0])
                dgi, bdgi = dg[fc % 2], bdg[fc % 2]
                k.tt(V, dgi, idb[:, :].unsqueeze(1).to_broadcast([128, 4, 128]),
                     cw[:, fc, :].unsqueeze(2).to_broadcast([128, 4, 128]), ALU.mult, [bc, bcw], [bdgi])
                for t in range(4):
                    k.mm(ps, dgi[:, t, :], xpre[:, fc, t:t + 256], t == 0, t == 3, [bdgi, bxpre], [bps])
                k.actf(xbcT[:, fc, :], ps, AF.Silu, [bps, bcw], [bxbc], bias=cbias[:, fc:fc + 1])

            for j in range(2):
                c = 2 * sc + j
                t0 = j * 128
                if c == 0:
                    for g in range(4):
                        k.ms(PL, hs[:, g * 512:(g + 1) * 512], 0.0, [bhs[g]])
                        k.ms(PL, hb[:, g * 512:(g + 1) * 512], 0.0, [bhb[g]])
                if samp:
                    b = c - NCHP
                    k.dma(SP, hso, self.st_ssm[L, b].rearrange("(a p) n -> p a n", p=128), [bext], [bhso] + bszg, bhso)
                    for half in range(2):
                        for bl in range(8):
                            blk = half * 8 + bl
                            k.tr(PA[:, bl * 128:(bl + 1) * 128], hso[:, blk, :], idf[:, :], [bhso, bc] + bszg, [bPA[bl // 4]])
                        for gg in range(2):
                            g = half * 2 + gg
                            k.cp(V, hs[:, g * 512:(g + 1) * 512], PA[:, gg * 512:(gg + 1) * 512], [bPA[gg]], [bhs[g]])
                            k.cp(ACT, hb[:, g * 512:(g + 1) * 512], PA[:, gg * 512:(gg + 1) * 512], [bPA[gg]], [bhb[g]])
                for kc in range(8):
                    k.mm(PC[:, 0:32], hT[:, kc, t0:t0 + 128], Win[:, kc, 5120:5152], kc == 0, kc == 7, [bhT[j], bWin], [bPC[0]])
                k.tt(V, dt_raw, PC[:, 0:32], dtb, ALU.add, [bPC[0], bsm], [bsmt])
                k.actf(dt_abs, dt_raw, AF.Abs, [bsmt], [bsmt])
                k.actf(dt_abs, dt_abs, AF.Exp, [bsmt], [bsmt], scale=-1.0)
                k.actf(dt_abs, dt_abs, AF.Ln, [bsmt], [bsmt], bias=1.0, scale=1.0)
                k.stt(V, dt_, dt_raw, 0.0, dt_abs, ALU.max, ALU.add, [bsmt], [bsmt])
                if samp:
                    k.ts(V, dt_, dt_, self.onehot0[:, 0:1], None, ALU.mult, None, [bsmt, bc], [bsmt])
                k.tt(V, dA, dt_, abc, ALU.mult, [bsmt, bsm], [bsmt])
                k.cp(V, dAhl[:, 0, :], dA, [bsmt], [bdA])
                k.tt(V, dAhl[:, 1, :], dA, dAhl[:, 0, :], ALU.subtract, [bsmt, bdA], [bdA])
                for i, lhs in enumerate((self.utri, self.onesb)):
                    o = PC[:, 32 + 32 * i:64 + 32 * i]
                    k.mm(o, lhs[:, :], dAhl[:, 0, :], True, False, [bc, bdA], [bPC[0]])
                    k.mm(o, lhs[:, :], dAhl[:, 1, :], False, True, [bc, bdA], [bPC[0]])
                k.cp(V, acs, PC[:, 32:64], [bPC[0]], [bsmt])
                k.cp(V, alast, PC[:, 64:96], [bPC[0]], [bsmt])
                k.actf(ee, acs, AF.Exp, [bsmt], [bsmt])
                k.actf(cd, alast, AF.Exp, [bsmt], [bsmt])
                k.tt(V, te, alast, acs, ALU.subtract, [bsmt], [bsmt])
                k.actf(te, te, AF.Exp, [bsmt], [bsmt])
                k.tt(V, dts, dt_, te, ALU.mult, [bsmt], [bsmt])
                for fc in range(16):
                    k.tr(PYb[:, fc * 128:(fc + 1) * 128], xbcT[:, fc, t0:t0 + 128], idb[:, :], [bxbc, bc], [bPY[fc // 8]])
                xv = PYb[:, 0:2048].rearrange("p (h d) -> p h d", h=NH)
                k.tt(V, xdt.rearrange("p (h d) -> p h d", h=NH), xv, dt_.unsqueeze(2).to_broadcast([128, NH, 64]),
                     ALU.mult, bPY[0:2] + [bsmt], [bxdt])
                k.tt(V, xdts.rearrange("p (h d) -> p h d", h=NH), xv, dts.unsqueeze(2).to_broadcast([128, NH, 64]),
                     ALU.mult, bPY[0:2] + [bsmt], [bxdts])
                for g in range(4):
                    k.tr(PCb[:, 512 + g * 128:512 + (g + 1) * 128], xbcT[:, 16 + g, t0:t0 + 128], idb[:, :], [bxbc, bc], [bPC[1]])
                k.cp(ACT, Btok, PCb[:, 512:1024], [bPC[1]], [bBtok])
                for half in range(2):
                    for q in range(2):
                        col = (2 * half + q) * 512
                        for kc in range(8):
                            k.mm(PA[:, q * 512:(q + 1) * 512], hT[:, kc, t0:t0 + 128], Win[:, kc, col:col + 512],
                                 kc == 0, kc == 7, [bhT[j], bWin], [bPA[q]])
                    k.actf(sz[:, half * 1024:(half + 1) * 1024], PA[:, :], AF.Silu, bPA, [bsz, bszg[2 * half], bszg[2 * half + 1]])
                for g in range(4):
                    hsg = hs[:, g * 512:(g + 1) * 512]
                    k.tt(V, hsg.rearrange("p (h d) -> p h d", h=8), hsg.rearrange("p (h d) -> p h d", h=8),
                         cd[:, g * 8:(g + 1) * 8].unsqueeze(2).to_broadcast([128, 8, 64]), ALU.mult, [bhs[g], bsmt], [bhs[g]])

                def rt(g):
                    i = g % 2
                    k.tt(V, Rt[i], self.utri[:, :].unsqueeze(1).to_broadcast([128, 8, 128]),
                         dAhl[:, 0, g * 8:(g + 1) * 8].unsqueeze(2).to_broadcast([128, 8, 128]), ALU.mult, [bc, bdA], [bR[i]])

                rt(0)
                for g in range(4):
                    i = g % 2
                    sps, bsp = (PA, bPA) if i == 0 else (PY[:, 1024:2048], bPY[2:4])
                    for half in range(2):
                        o = sps[:, half * 512:(half + 1) * 512]
                        k.mm(o, self.ltri[:, :], Rt[i][:, half * 4:(half + 1) * 4, :], True, True, [bc, bR[i]], [bsp[half]])
                    cps, bcp = (PB[:, 0:128], bPB[0]) if i == 0 else (PC[:, 128:256], bPC[0])
                    k.mm(cps, xbcT[:, 16 + g, t0:t0 + 128], xbcT[:, 20 + g, t0:t0 + 128], True, True, [bxbc], [bcp])
                    if g + 1 < 4:
                        rt(g + 1)
                    k.actf(E[i], sps[:, 0:1024].rearrange("p (a b) -> p a b", a=8), AF.Exp, list(bsp), [bE[i]])
                    k.tt(V, cbm[i], cps, self.causT[:, :], ALU.mult, [bcp, bc], [bcbm[i]])
                    k.tt(V, Mg[:, g * 8:(g + 1) * 8, :], E[i], cbm[i].unsqueeze(1).to_broadcast([128, 8, 128]), ALU.mult,
                         [bE[i], bcbm[i]], [bMg[g]])
                for g in range(4):
                    hsg = hs[:, g * 512:(g + 1) * 512]
                    k.mm(PB[:, :], xbcT[:, 20 + g, t0:t0 + 128], hb[:, g * 512:(g + 1) * 512], True, True, [bxbc, bhb[g]], bPB)
                    k.mm(PC[:, :], Btok[:, g * 128:(g + 1) * 128], xdts[:, g * 512:(g + 1) * 512], True, True, [bBtok, bxdts], bPC)
                    k.tt(V, yo.rearrange("p (h d) -> p h d", h=8), PB[:, :].rearrange("p (h d) -> p h d", h=8),
                         ee[:, g * 8:(g + 1) * 8].unsqueeze(2).to_broadcast([128, 8, 64]), ALU.mult, bPB + [bsmt], [byo])
                    for pr in range(4):
                        fc = g * 4 + pr
                        k.mm(PY[:, fc * 128:(fc + 1) * 128], xbcT[:, fc, t0:t0 + 128], diagD[:, fc, :], pr == 0, False,
                             [bxbc, bdiagD], [bPY[g]])
                    for hh in range(8):
                        h = g * 8 + hh
                        k.mm(PY[:, h * 64:(h + 1) * 64], Mg[:, h, :], xdt[:, h * 64:(h + 1) * 64], False, False,
                             [bMg[g], bxdt], [bPY[g]])
                    k.mm(PY[:, g * 512:(g + 1) * 512], idb[:, :], yo, False, True, [bc, byo], [bPY[g]])
                    k.tt(V, hsg, hsg, PC[:, :], ALU.add, [bhs[g]] + bPC, [bhs[g]])
                    k.cp(ACT, hb[:, g * 512:(g + 1) * 512], hsg, [bhs[g]], [bhb[g]])
                    szg = sz[:, g * 512:(g + 1) * 512]
                    if g == 0:
                        k.ms(PL, ss[:, 0:4], 0.0, [bss])
                    k.tt(V, szg, PY[:, g * 512:(g + 1) * 512], szg, ALU.mult, [bPY[g], bszg[g]], [bszg[g]])
                    k.actf(junk, szg, AF.Square, [bszg[g], bss], [bjunk, bss], accum=ss[:, g:g + 1])
                k.ts(V, ss[:, 4:8], ss[:, 0:4], 1.0 / 512, RMS_EPS, ALU.mult, ALU.add, [bss], [bss])
                k.actf(ss[:, 8:12], ss[:, 4:8], AF.Ln, [bss], [bss])
                k.actf(ss[:, 12:16], ss[:, 8:12], AF.Exp, [bss], [bss], scale=-0.5)
                for g in range(4):
                    szg = sz[:, g * 512:(g + 1) * 512]
                    if g % 2 == 0:
                        k.actf(gn[:, g * 512:(g + 1) * 512], szg, AF.Copy, [bszg[g], bss], [bgn], scale=ss[:, 12 + g:13 + g])
                    else:
                        k.ts(V, gn[:, g * 512:(g + 1) * 512], szg, ss[:, 12 + g:13 + g], None, ALU.mult, None, [bszg[g], bss], [bgn])
                k.dma(PL, self.G[c * 128:(c + 1) * 128, :], gn, [bgn], [self.bG[c]], bgn)
                if c == NCHP - 1:
                    state_out(self.ssm_p[L])
                if samp:
                    state_out(self.ssm_s[L, c - NCHP])

    def a_sweep2(self, L):
        k = self.k
        A = self.arena
        A.reset()
        PE, ACT, V, PL, SP = k.pe, k.act, k.dve, k.pool, k.sp
        bext, bc = self.bext, self.bconst
        PY, PA = self.PY, self.PA
        bPY, bPA = self.bPY, self.bPA
        PAb = PA[:, :].bitcast(BF16)
        if L == 1:
            self.hTall = A.alloc((8, SEQ), BF16)
        else:
            WinN = A.alloc((8, AIN), BF16); bWinN = k.buf("Win")
            self.win_prefetched = True
        Wout = A.alloc((16, D), BF16); bWout = k.buf("Wout")
        wst = [A.alloc((D,), F32) for _ in range(2)]; bwst = [k.buf(f"wst{i}") for i in range(2)]
        nw = A.alloc((16,), F32); bnw = k.buf("nw")
        gnt = [A.alloc((DI,), BF16) for _ in range(2)]; bgnt = [k.buf(f"gnt{i}") for i in range(2)]
        gT = [A.alloc((16, 128), BF16) for _ in range(2)]; bgT = [k.buf(f"gT{i}") for i in range(2)]
        hin = [A.alloc((D,), F32) for _ in range(2)]; bhin = [k.buf(f"hin{i}") for i in range(2)]
        r = [A.alloc((D,), F32) for _ in range(2)]; br = [k.buf(f"r{i}") for i in range(2)]
        stat = [A.alloc((8,), F32) for _ in range(2)]; bstat = [k.buf(f"stat{i}") for i in range(2)]
        junk = [A.alloc((D,), BF16) for _ in range(2)]; bjunk = [k.buf(f"junk2{i}") for i in range(2)]
        hTl = A.alloc((8, 128), BF16); bhTl = k.buf("hTl2")
        self.load_ln(L)
        nwraw = A.alloc((128,), F32, parts=16)
        k.dma(SP, nwraw, self.a_nw[L].rearrange("(c p) -> c p", p=128), [bext], [bnw], bnw)
        k.tr(PA[:, 0:16], nwraw, self.idf[0:16, 0:16], [bnw, bc], [bPA[0]])
        k.cp(V, nw, PA[:, 0:16], [bPA[0]], [bnw])
        for fc in range(16):
            i = fc % 2
            k.dma(SP, wst[i], self.a_out[L, fc * 128:(fc + 1) * 128, :], [bext], [bwst[i]], bwst[i])
            k.actf(Wout[:, fc, :], wst[i], AF.Copy, [bwst[i], bnw], [bWout], scale=nw[:, fc:fc + 1])
        for c in range(NCH):
            i = c % 2
            samp = c >= NCHP
            if L == 0 and c % 2 == 0 and c // 2 < 8:
                kc = c // 2
                k.dma(PL, WinN[:, kc, :], self.a_in[1, kc * 128:(kc + 1) * 128, :], [bext], [bWinN], bWinN)
            k.dma(SP, gnt[i], self.G[c * 128:(c + 1) * 128, :], [self.bG[c]], [bgnt[i]], bgnt[i])
            if L == 0 and not samp:
                k.dma(SP, hin[i], self.xp[c * 128:(c + 1) * 128, :], [bext], [bhin[i]], bhin[i])
            else:
                k.dma(SP, hin[i], self.S[c * 128:(c + 1) * 128, :], [self.bS[c]], [bhin[i]], bhin[i])
            for fc in range(16):
                k.tr(PAb[:, fc * 128:(fc + 1) * 128], gnt[i][:, fc * 128:(fc + 1) * 128], self.idb[:, :], [bgnt[i], bc], [bPA[fc // 8]])
            k.cp(ACT, gT[i], PAb.rearrange("p (a b) -> p a b", a=16), bPA, [bgT[i]])
            for half in range(2):
                for fc in range(16):
                    k.mm(PY[:, i * 1024 + half * 512:i * 1024 + (half + 1) * 512], gT[i][:, fc, :], Wout[:, fc, half * 512:(half + 1) * 512],
                         fc == 0, fc == 15, [bgT[i], bWout], [bPY[2 * i + half]])
            self.resid_ln(PY[:, i * 1024:(i + 1) * 1024], bPY[2 * i:2 * i + 2], hin[i], bhin[i], r[i], br[i], stat[i], bstat[i], junk[i], bjunk[i])
            k.dma(PL, self.S[c * 128:(c + 1) * 128, :], r[i], [br[i]], [self.bS[c]], br[i])
            if L == 1:
                for kc in range(8):
                    k.tr(PA[:, kc * 128:(kc + 1) * 128], r[i][:, kc * 128:(kc + 1) * 128], self.idf[:, :], [br[i], bc], [bPA[kc // 4]])
                if not samp:
                    k.cp(ACT, self.hTall[:, :, c * 128:(c + 1) * 128], PA[:, :].rearrange("p (a b) -> p a b", a=8), bPA, [self.bhT[c]])
                else:
                    b = c - NCHP
                    k.cp(ACT, self.hsT[:, :, b:b + 1], PA[:, :].rearrange("p (a b) -> p a b", a=8)[:, :, 0:1], bPA, [self.bhsT])


    def kv_phase(self):
        k = self.k
        A = self.arena
        A.reset()
        PE, ACT, V, PL, SP = k.pe, k.act, k.dve, k.pool, k.sp
        bext, bc = self.bext, self.bconst
        PY, PA, PB, PC = self.PY, self.PA, self.PB, self.PC
        bPY, bPA, bPB, bPC = self.bPY, self.bPA, self.bPB, self.bPC
        self.hTall = A.alloc((8, SEQ), BF16)
        Wkv = A.alloc((8, 6144), BF16); bW = k.buf("Wkv")
        KT = [A.alloc((8, 512), BF16) for _ in range(2)]; bKTt = [k.buf(f"KTt{i}") for i in range(2)]
        kvf = [A.alloc((2048,), F32) for _ in range(2)]; bkvf = [k.buf(f"kvf{i}") for i in range(2)]
        Vb = [A.alloc((D,), BF16) for _ in range(2)]; bVb = [k.buf(f"Vb{i}") for i in range(2)]
        kvs_sb = A.alloc((6144,), F32, parts=NSAMP); bkvs = k.buf("kvs_sb")
        for kc in range(8):
            k.dma(PL, Wkv[:, kc, :], self.kvw[kc * 128:(kc + 1) * 128, :], [bext], [bW], bW)
        allhT = self.bhT
        it = 0
        for g in range(3):
            dil = DILS[g]
            n = SEQ // dil
            W = WINS[g]
            for q4 in range(4):
                pb = q4 * 512
                kt = KT[q4 % 2]; bkt = bKTt[q4 % 2]
                for fc in range(8):
                    ps, bps = (PB[:, :], bPB[0]) if fc % 2 == 0 else (PC[:, :], bPC[0])
                    for kc in range(8):
                        k.mm(ps, Wkv[:, kc, g * 1024 + fc * 128:g * 1024 + (fc + 1) * 128], self.tok_ap(g, kc, pb, 512),
                             kc == 0, kc == 7, [bW] + allhT, [bps])
                    k.cp(V if fc % 2 == 0 else ACT, kt[:, fc, :], ps, [bps], [bkt])
                k.dma(PL, self.KTs[g].rearrange("(c p) t -> p c t", p=128)[:, :, pb:pb + 512], kt, [bkt], [self.bKT[g]], bkt)
                for tt in range(4):
                    pos0 = pb + tt * 128
                    i0 = pos0 % n
                    need_k = (i0 * dil + (dil - 1)) >= SEQ - W
                    i2 = it % 2
                    it += 1
                    for half in range(2):
                        col = 3072 + g * 1024 + half * 512
                        for kc in range(8):
                            k.mm(PA[:, half * 512:(half + 1) * 512], self.tok_ap(g, kc, pos0, 128), Wkv[:, kc, col:col + 512],
                                 kc == 0, kc == 7, [bW] + allhT, [bPA[half]])
                    k.cp(ACT, Vb[i2], PA[:, :], bPA, [bVb[i2]])
                    k.dma(PL, self.Vs[g][pos0:pos0 + 128, :], Vb[i2], [bVb[i2]], [self.bVs[g]], bVb[i2])
                    if need_k:
                        k.cp(V, kvf[i2][:, 1024:2048], PA[:, :], bPA, [bkvf[i2]])
                        for half in range(2):
                            col = g * 1024 + half * 512
                            for kc in range(8):
                                k.mm(PY[:, half * 512:(half + 1) * 512], self.tok_ap(g, kc, pos0, 128), Wkv[:, kc, col:col + 512],
                                     kc == 0, kc == 7, [bW] + allhT, [bPY[half]])
                        k.cp(V, kvf[i2][:, 0:1024], PY[:, 0:1024], bPY[0:2], [bkvf[i2]])
                        r0 = pos0 // n
                        tok0 = i0 * dil + r0
                        if dil == 1:
                            dst = self.kvp[g][tok0 - (SEQ - W):tok0 - (SEQ - W) + 128, :]
                        else:
                            ib = (SEQ - W) // dil
                            dst = self.kvp[g].rearrange("(i r) c -> r i c", r=dil)[r0, i0 - ib:i0 - ib + 128, :]
                        k.dma(PL, dst, kvf[i2], [bkvf[i2]], [bext], bkvf[i2])
        for cg in range(12):
            ps, bps = (PB[0:NSAMP, :], bPB[0]) if cg % 2 == 0 else (PC[0:NSAMP, :], bPC[0])
            for kc in range(8):
                k.mm(ps, self.hsT[:, kc, :], Wkv[:, kc, cg * 512:(cg + 1) * 512], kc == 0, kc == 7, [bW, self.bhsT], [bps])
            k.cp(V if cg % 2 == 0 else ACT, kvs_sb[:, cg * 512:(cg + 1) * 512], ps, [bps], [bkvs])
        k.dma(PL, self.KVS, kvs_sb, [bkvs], [self.bKVS], bkvs)
        for g in range(3):
            W = WINS[g]
            for b in range(NSAMP):
                k.dma(PL, self.kvs[g][b, W - 1:W, 0:1024], kvs_sb[b:b + 1, g * 1024:(g + 1) * 1024], [bkvs], [bext], bkvs)
                k.dma(PL, self.kvs[g][b, W - 1:W, 1024:2048], kvs_sb[b:b + 1, 3072 + g * 1024:3072 + (g + 1) * 1024], [bkvs], [bext], bkvs)
        for c in range(24):
            k.tr(PA[:, c * 4:(c + 1) * 4], kvs_sb[:, c * 128:(c + 1) * 128], self.idf[0:NSAMP, 0:NSAMP], [bkvs, bc], [bPA[0]])
        k.cp(V, self.knewT[:, :, :], PA[:, 0:96].rearrange("p (c b) -> p c b", c=24), [bPA[0]], [self.bknew])

    def b_phase(self, j):
        k = self.k
        A = self.arena
        A.reset()
        layer = 2 + j
        last = (j == 1)
        PE, ACT, V, PL, SP = k.pe, k.act, k.dve, k.pool, k.sp
        bext, bc = self.bext, self.bconst
        PY, PA, PB, PC = self.PY, self.PA, self.PB, self.PC
        bPY, bPA, bPB, bPC = self.bPY, self.bPA, self.bPB, self.bPC
        PYb = PY[:, :].bitcast(BF16)
        PAb = PA[:, :].bitcast(BF16)
        idf, idb = self.idf, self.idb
        self.hTall = A.alloc((8, SEQ), BF16)
        allhT = self.bhT
        Wb = A.alloc((8, 4096), BF16); bWb = k.buf("Wbin")
        Wo = A.alloc((8, D), BF16); bWo = k.buf("Wbout")
        AB = A.alloc((16, 256), F32); bAB = k.buf("AB")
        DM = A.alloc((3, 256), F32); bDM = k.buf("DM")
        dmf = A.alloc((256,), F32); bdmf = k.buf("dmf")
        mark = A.off
        QT = A.alloc((8, 512), BF16); bQT = k.buf("QT")
        ktw = [A.alloc((8, 256), BF16) for _ in range(2)]; bktw = [k.buf(f"ktw{i}") for i in range(2)]
        vw = [A.alloc((2, D), BF16) for _ in range(2)]; bvw = [k.buf(f"vw{i}") for i in range(2)]
        Sb = [A.alloc((4, 256), F32) for _ in range(2)]; bSb = [k.buf(f"Sb{i}") for i in range(2)]
        Pt = [A.alloc((4, 256), BF16) for _ in range(2)]; bP = [k.buf(f"P{i}") for i in range(2)]
        PT = [A.alloc((4, 2, 128), BF16) for _ in range(2)]; bPT = [k.buf(f"PT{i}") for i in range(2)]
        accrow = [A.alloc((ACCW,), F32) for _ in range(2)]; bacc = [k.buf(f"accrow{i}") for i in range(2)]
        negm = A.alloc((16,), F32); bnegm = k.buf("negm")
        prod = A.alloc((8, 128), BF16); bprod = k.buf("prod")
        ind2 = A.alloc((2,), BF16)
        k.ms(PL, ind2, 0.0, [bc])
        k.ms(PL, ind2[0:64, 0:1], 1.0, [bc])
        k.ms(PL, ind2[64:128, 1:2], 1.0, [bc])
        sr = A.alloc((4096,), BF16); bsr = k.buf("sr")
        ktws = sr[:, 0:2048].rearrange("p (a b) -> p a b", a=8); bktws = bsr
        vws = sr[:, 2048:4096].rearrange("p (a b) -> p a b", a=2); bvws = bsr
        QTb = [QT, sr.rearrange("p (a b) -> p a b", a=8)]; bQTb = [bQT, bsr]
        Kc = A.alloc((D,), BF16); bKc = k.buf("Kc")
        QTs = A.alloc((8, 128), BF16); bQTs = k.buf("QTs")
        qsT = A.alloc((24, NSAMP), BF16); bqsT = k.buf("qsT")

        self.load_ln(layer)
        for kc in range(8):
            k.dma(PL, Wb[:, kc, :], self.b_in[j, kc * 128:(kc + 1) * 128, :], [bext], [bWb], bWb)
            k.dma(PL, Wo[:, kc, :], self.b_out[j, kc * 128:(kc + 1) * 128, :], [bext], [bWo], bWo)
        k.op(PL, lambda e: e.iota(dmf, pattern=[[1, 256]], base=-128, channel_multiplier=-1,
                                  allow_small_or_imprecise_dtypes=True), [], [bdmf])
        for g in range(3):
            k.ts(PL, DM[:, g, :], dmf, float(DILS[g]), None, ALU.mult, None, [bdmf], [bDM])
            k.op(PL, lambda e, g=g: e.affine_select(out=DM[:, g, :], in_=DM[:, g, :], pattern=[[-1, 256]], compare_op=ALU.is_ge,
                                                   fill=-32768.0, base=128, channel_multiplier=1), [bDM], [bDM])
            k.op(PL, lambda e, g=g: e.affine_select(out=DM[:, g, :], in_=DM[:, g, :], pattern=[[1, 256]], compare_op=ALU.is_ge,
                                                   fill=-32768.0, base=0, channel_multiplier=-1), [bDM], [bDM])
        k.ms(PL, QTs, 0.0, [bQTs])
        for c in range(24):
            for kc in range(8):
                k.mm(PB[:, c * 4:(c + 1) * 4], Wb[:, kc, c * 128:(c + 1) * 128], self.hsT[:, kc, :], kc == 0, kc == 7,
                     [bWb, self.bhsT], [bPB[0]])
        k.actf(qsT, PB[:, 0:96].rearrange("p (c b) -> p c b", c=24), AF.Copy, [bPB[0]], [bqsT], scale=0.125)

        def attn_tile(g, qt3, bq, kt, bkt, vt, bvt, first_block, arow, barow):
            k0 = 128 if first_block else 0
            nk = 256 - k0
            nkb = nk // 128
            k.tt(V, prod, qt3, kt[:, :, 128:256], ALU.mult, [bq, bkt], [bprod])
            for fc in range(8):
                k.mm(PY[:, 2 * fc:2 * fc + 2], prod[:, fc, :], ind2[:, :], True, True, [bprod, bc], [bPY[0]])
            k.cp(V, arow[:, 1040:1056], PY[:, 0:16], [bPY[0]], [barow])
            k.ts(V, negm, PY[:, 0:16], -1.0, None, ALU.mult, None, [bPY[0]], [bnegm])
            k.ms(PL, arow[:, 1024:1040], 0.0, [barow])

            def stage_qk(hg):
                i = hg % 2
                for hh in range(4):
                    h = hg * 4 + hh
                    fc, hp = h // 2, h % 2
                    bank = 2 * i + hh % 2
                    c0 = bank * 512 + (hh // 2) * 256
                    k.mm(PY[:, c0:c0 + nk], qt3[hp * 64:(hp + 1) * 64, fc, :], kt[hp * 64:(hp + 1) * 64, fc, k0:256], True, True,
                         [bq, bkt], [bPY[bank]])
                p4 = PY[:, 2 * i * 512:(2 * i + 2) * 512].rearrange("p (bank half kk) -> p bank half kk", bank=2, half=2)[:, :, :, 0:nk]
                s4 = Sb[i].rearrange("p (half bank) kk -> p bank half kk", bank=2)[:, :, :, 0:nk]
                a4 = AB[:, hg * 4:(hg + 1) * 4, :].rearrange("p (half bank) kk -> p bank half kk", bank=2)[:, :, :, k0:256]
                k.tt(V, s4, p4, a4, ALU.add, [bPY[2 * i], bPY[2 * i + 1], bAB], [bSb[i]])
                for hh in range(4):
                    h = hg * 4 + hh
                    k.actf(Pt[i][:, hh, 0:nk], Sb[i][:, hh, 0:nk], AF.Exp, [bSb[i], bnegm, barow], [bP[i], barow], bias=negm[:, h:h + 1], scale=1.0,
                           accum=arow[:, 1024 + h:1025 + h])

            def stage_pv(hg):
                i = hg % 2
                ptp, bptp = PB[:, :].bitcast(BF16), bPB[0]
                for hh in range(4):
                    for kb in range(nkb):
                        c0 = (hh * nkb + kb) * 128
                        k.tr(ptp[:, c0:c0 + 128], Pt[i][:, hh, kb * 128:(kb + 1) * 128], idb[:, :], [bP[i], bc], [bptp])
                k.cp(V, PT[i][:, :, 0:nkb, :], ptp[:, 0:4 * nkb * 128].rearrange("p (a b c) -> p a b c", a=4, b=nkb), [bptp], [bPT[i]])
                for hh in range(4):
                    h = hg * 4 + hh
                    for kb in range(nkb):
                        k.mm(PA[:, h * 64:(h + 1) * 64], PT[i][:, hh, kb, :], vt[:, k0 // 128 + kb, h * 64:(h + 1) * 64],
                             kb == 0, kb == nkb - 1, [bPT[i], bvt], [bPA[h // 8]])

            stage_qk(0)
            for hg in range(4):
                if hg + 1 < 4:
                    stage_qk(hg + 1)
                stage_pv(hg)
            k.cp(ACT, arow[:, 0:512], PA[:, 0:512], [bPA[0]], [barow])
            k.cp(V, arow[:, 512:1024], PA[:, 512:1024], [bPA[1]], [barow])

        def build_ab(g):
            for h in range(16):
                k.ts(V, AB[:, h, :], DM[:, g, :], float(2.0 ** (-(h + 1) / 2.0)), None, ALU.mult, None, [bDM], [bAB])

        groups = [(g, q4) for g in range(3) for q4 in range(4)]

        def qproj_part(idx, part):
            g, q4 = groups[idx]
            qt, bqt = QTb[idx % 2], bQTb[idx % 2]
            for fc in (2 * part, 2 * part + 1):
                for kc in range(8):
                    k.mm(PC[:, :], Wb[:, kc, g * 1024 + fc * 128:g * 1024 + (fc + 1) * 128], self.tok_ap(g, kc, q4 * 512, 512),
                         kc == 0, kc == 7, [bWb] + allhT, [bPC[0]])
                k.ts(V, qt[:, fc, :], PC[:, :], 0.125, None, ALU.mult, None, [bPC[0]], [bqt])

        it = 0
        for part in range(4):
            qproj_part(0, part)
        for idx, (g, q4) in enumerate(groups):
            dil = DILS[g]
            n = SEQ // dil
            pb = q4 * 512
            qt, bqt = QTb[idx % 2], bQTb[idx % 2]
            if q4 == 0:
                build_ab(g)
            for tt in range(4):
                pos0 = pb + tt * 128
                i0 = pos0 % n
                first = (i0 == 0)
                i2 = it % 2
                it += 1
                ktv = self.KTs[g].rearrange("(c p) t -> p c t", p=128)
                if first:
                    k.dma(SP, ktw[i2][:, :, 128:256], ktv[:, :, pos0:pos0 + 128], [self.bKT[g]], [bktw[i2]], bktw[i2])
                    k.dma(SP, vw[i2][:, 1, :], self.Vs[g][pos0:pos0 + 128, :], [self.bVs[g]], [bvw[i2]], bvw[i2])
                else:
                    k.dma(SP, ktw[i2], ktv[:, :, pos0 - 128:pos0 + 128], [self.bKT[g]], [bktw[i2]], bktw[i2])
                    k.dma(SP, vw[i2], self.Vs[g][pos0 - 128:pos0 + 128, :].rearrange("(a p) c -> p a c", p=128),
                          [self.bVs[g]], [bvw[i2]], bvw[i2])
                if idx + 1 < len(groups):
                    qproj_part(idx + 1, tt)
                attn_tile(g, qt[:, :, tt * 128:(tt + 1) * 128], bqt, ktw[i2], bktw[i2], vw[i2], bvw[i2],
                          first, accrow[i2], bacc[i2])
                k.dma(PL, self.row_ap(self.ACC[g][0:SEQ, :], g, pos0), accrow[i2], [bacc[i2]], [self.bACC[g]], bacc[i2])
        k.ms(PL, ktws, 0.0, [bktws])
        k.ms(PL, vws, 0.0, [bvws])
        for g in range(3):
            dil = DILS[g]
            build_ab(g)
            for b in range(NSAMP):
                i2 = it % 2
                it += 1
                cv = self.ck[g][b].rearrange("(i r) c -> r i c", r=dil)[0]
                k.dma(PL, Kc, cv[:, 0:1024], [bext], [bKc], bKc)
                for half in range(2):
                    for f4 in range(4):
                        fc = half * 4 + f4
                        k.tr(PYb[:, 2048 + half * 1024 + f4 * 128:2048 + half * 1024 + (f4 + 1) * 128], Kc[:, fc * 128:(fc + 1) * 128],
                             idb[:, :], [bKc, bc], [bPY[2 + half]])
                    k.cp(V if half else ACT, ktws[:, half * 4:(half + 1) * 4, 127:255],
                         PYb[:, 2048 + half * 1024:2048 + half * 1024 + 512].rearrange("p (a b) -> p a b", a=4), [bPY[2 + half]], [bktws])
                k.cp(V, ktws[:, :, 255:256], self.knewT[:, g * 8:(g + 1) * 8, b:b + 1], [self.bknew], [bktws])
                k.dma(PL, vws[127:128, 0, :], cv[0:1, 1024:2048], [bext], [bvws], bvws)
                k.dma(PL, vws[0:127, 1, :], cv[1:128, 1024:2048], [bext], [bvws], bvws)
                k.dma(PL, vws[127:128, 1, :], self.KVS[b:b + 1, 3072 + g * 1024:3072 + (g + 1) * 1024], [self.bKVS], [bvws], bvws)
                k.cp(V, QTs[:, :, 127:128], qsT[:, g * 8:(g + 1) * 8, b:b + 1], [bqsT], [bQTs])
                attn_tile(g, QTs, bQTs, ktws, bktws, vws, bvws, False, accrow[i2], bacc[i2])
                k.dma(PL, self.ACC[g][SEQ + b:SEQ + b + 1, :], accrow[i2][127:128, :], [bacc[i2]], [self.bACC[g]], bacc[i2])

        k.barrier()
        A.off = mark
        sets = []
        for i in range(2):
            sets.append((A.alloc((3, ACCW), F32), k.buf(f"acc3{i}"), A.alloc((16 * 12,), F32), k.buf(f"msm{i}"),
                         A.alloc((D,), F32), k.buf(f"sg{i}"), A.alloc((D,), BF16), k.buf(f"og{i}"),
                         A.alloc((8, 128), BF16), k.buf(f"ogT{i}"), A.alloc((D,), F32), k.buf(f"hin{i}"),
                         A.alloc((D,), F32), k.buf(f"r{i}"), A.alloc((8,), F32), k.buf(f"stat{i}"),
                         A.alloc((D,), BF16), k.buf(f"junk2{i}")))
        for c in range(NCHP + 1):
            (acc3, bacc3, msm, bmsm, sg, bsg, og, bog, ogT, bogT, hin, bhin, r, br, stat, bstat, junk, bjunk) = sets[c % 2]
            samp = (c == NCHP)
            np_ = NSAMP if samp else 128
            sl = slice(0, np_)
            for g in range(3):
                src = self.ACC[g][SEQ:SEQ + NSAMP, :] if samp else self.ACC[g][c * 128:(c + 1) * 128, :]
                k.dma(SP, acc3[sl, g, :], src, [self.bACC[g]], [bacc3], bacc3)
            if samp:
                k.dma(SP, hin[sl, :], self.S[SEQ:NCH * 128, :].rearrange("(b p) d -> p b d", p=128)[0], [self.bS[c] for c in range(NCHP, NCH)],
                      [bhin], bhin)
            else:
                k.dma(SP, hin[sl, :], self.S[c * 128:(c + 1) * 128, :], [self.bS[c]], [bhin], bhin)
            m3 = acc3[sl, :, 1040:1056]
            s3 = acc3[sl, :, 1024:1040]
            mmax = msm[sl, 0:16]
            w3 = msm[sl, 16:64].rearrange("p (g h) -> p g h", g=3)
            den = msm[sl, 64:80]
            rden = msm[sl, 80:96]
            co3 = msm[sl, 96:144].rearrange("p (g h) -> p g h", g=3)
            ws3 = msm[sl, 144:192].rearrange("p (g h) -> p g h", g=3)
            k.tt(V, mmax, acc3[sl, 0, 1040:1056], acc3[sl, 1, 1040:1056], ALU.max, [bacc3], [bmsm])
            k.tt(V, mmax, mmax, acc3[sl, 2, 1040:1056], ALU.max, [bacc3, bmsm], [bmsm])
            k.tt(V, w3, m3, mmax.unsqueeze(1).to_broadcast([np_, 3, 16]), ALU.subtract, [bacc3, bmsm], [bmsm])
            k.actf(w3, w3, AF.Exp, [bmsm], [bmsm])
            k.tt(V, ws3, w3, s3, ALU.mult, [bmsm, bacc3], [bmsm])
            k.tt(V, den, ws3[:, 0, :], ws3[:, 1, :], ALU.add, [bmsm], [bmsm])
            k.tt(V, den, den, ws3[:, 2, :], ALU.add, [bmsm], [bmsm])
            k.recip(rden, den, [bmsm], [bmsm])
            k.tt(V, co3, w3, rden.unsqueeze(1).to_broadcast([np_, 3, 16]), ALU.mult, [bmsm], [bmsm])
            for g in range(3):
                og3 = acc3[sl, g, 0:1024].rearrange("p (h d) -> p h d", h=16)
                k.tt(V, og3, og3, co3[:, g, :].unsqueeze(2).to_broadcast([np_, 16, 64]), ALU.mult, [bacc3, bmsm], [bacc3])
            k.tt(V, acc3[sl, 0, 0:1024], acc3[sl, 0, 0:1024], acc3[sl, 1, 0:1024], ALU.add, [bacc3], [bacc3])
            k.tt(V, acc3[sl, 0, 0:1024], acc3[sl, 0, 0:1024], acc3[sl, 2, 0:1024], ALU.add, [bacc3], [bacc3])
            for half in range(2):
                col = 3072 + half * 512
                for kc in range(8):
                    lhs = self.hsT[:, kc, :] if samp else self.hTall[:, kc, c * 128:(c + 1) * 128]
                    k.mm(PY[sl, half * 512:(half + 1) * 512], lhs, Wb[:, kc, col:col + 512], kc == 0, kc == 7,
                         [bWb, self.bhsT if samp else allhT[c]], [bPY[half]])
            k.actf(sg[sl, :], PY[sl, 0:1024], AF.Silu, bPY[0:2], [bsg])
            k.tt(V, og[sl, :], acc3[sl, 0, 0:1024], sg[sl, :], ALU.mult, [bacc3, bsg], [bog])
            for fc in range(8):
                k.tr(PAb[:, fc * 128:fc * 128 + np_], og[sl, fc * 128:(fc + 1) * 128], idb[sl, sl], [bog, bc], [bPA[0]])
            k.cp(ACT, ogT[:, :, 0:np_], PAb[:, 0:1024].rearrange("p (a b) -> p a b", a=8)[:, :, 0:np_], [bPA[0]], [bogT])
            for half in range(2):
                for fc in range(8):
                    k.mm(PY[sl, 1024 + half * 512:1024 + (half + 1) * 512], ogT[:, fc, 0:np_], Wo[:, fc, half * 512:(half + 1) * 512],
                         fc == 0, fc == 7, [bogT, bWo], [bPY[2 + half]])
            self.resid_ln(PY[sl, 1024:2048], bPY[2:4], hin[sl, :], bhin, r[sl, :], br, stat[sl, :], bstat, junk[sl, :], bjunk, np_=np_)
            if last:
                dst = self.ys if samp else self.yp[c * 128:(c + 1) * 128, :]
                k.dma(PL, dst, r[sl, :], [br], [bext], br)
            else:
                if samp:
                    k.dma(PL, self.S[SEQ:NCH * 128, :].rearrange("(b p) d -> p b d", p=128)[0], r[sl, :], [br],
                          [self.bS[c] for c in range(NCHP, NCH)], br)
                else:
                    k.dma(PL, self.S[c * 128:(c + 1) * 128, :], r[sl, :], [br], [self.bS[c]], br)
                for kc in range(8):
                    k.tr(PA[:, 512 + kc * 64:512 + kc * 64 + np_] if False else PA[:, kc * 128:kc * 128 + np_],
                         r[sl, kc * 128:(kc + 1) * 128], idf[sl, sl], [br, bc], [bPA[kc // 4]])
                if samp:
                    k.cp(ACT, self.hsT[:, :, :], PA[:, :].rearrange("p (a b) -> p a b", a=8)[:, :, 0:NSAMP], bPA, [self.bhsT])
                else:
                    k.cp(ACT, self.hTall[:, :, c * 128:(c + 1) * 128], PA[:, :].rearrange("p (a b) -> p a b", a=8), bPA, [allhT[c]])

_IN_NAMES = ["x_prompt", "x_sample", "state_ssm", "state_conv", "cache_kv_w128", "cache_kv_w512", "cache_kv_w2048",
             "a_in_proj", "a_conv_w", "a_conv_b", "a_dt_bias", "a_log", "a_d", "a_norm_w", "a_out_proj",
             "kv_proj", "b_in_proj", "b_out_proj", "ln_g", "ln_b"]


def make_in_maps(inp):
    f = lambda a: np.ascontiguousarray(np.asarray(a, dtype=np.float32))
    maps = []
    for i in range(NCORES):
        s = slice(NSAMP * i, NSAMP * (i + 1))
        m = {
            "xp": f(inp["x_prompt"][i]),
            "xs": f(inp["x_sample"][s, 0]),
            "st_ssm": f(np.asarray(inp["state_ssm"])[:, s].reshape(2, NSAMP, DI, 128)),
            "st_conv": f(np.asarray(inp["state_conv"])[:, s]),
            "ck128": f(np.asarray(inp["cache_kv_w128"])[s].reshape(NSAMP, 128, 2048)),
            "ck512": f(np.asarray(inp["cache_kv_w512"])[s].reshape(NSAMP, 512, 2048)),
            "ck2048": f(np.asarray(inp["cache_kv_w2048"])[s].reshape(NSAMP, 2048, 2048)),
            "a_in": f(inp["a_in_proj"]), "a_cw": f(inp["a_conv_w"]), "a_cb": f(inp["a_conv_b"]),
            "a_dtb": f(inp["a_dt_bias"]), "a_log": f(inp["a_log"]), "a_d": f(inp["a_d"]),
            "a_nw": f(inp["a_norm_w"]), "a_out": f(inp["a_out_proj"]), "kvw": f(inp["kv_proj"]),
            "b_in": f(inp["b_in_proj"]), "b_out": f(inp["b_out_proj"]), "ln_g": f(inp["ln_g"]), "ln_b": f(inp["ln_b"]),
        }
        maps.append(m)
    return maps


def kernel(**inputs):
    prog = Prog()
    maps = make_in_maps(inputs)
    res = run_bass_kernel_spmd(prog.nc, maps, core_ids=list(range(NCORES)))
    R = res.results
    cat = lambda name: np.concatenate([np.asarray(r[name]) for r in R], axis=0)
    y_prompt = np.stack([R[i]["yp"] for i in range(NCORES)]).reshape(8, SEQ, D)
    y_sample = cat("ys").reshape(32, 1, D)
    ssm_prompt = np.stack([R[i]["ssm_p"] for i in range(NCORES)], axis=1).reshape(2, 8, 32, 64, 128)
    conv_prompt = np.stack([R[i]["conv_p"] for i in range(NCORES)], axis=1).reshape(2, 8, 3, CD)
    kvp = [np.stack([R[i][n] for i in range(NCORES)]).reshape(8, w, 2, 16, 64)
           for n, w in (("kv128_p", 128), ("kv512_p", 512), ("kv2048_p", 2048))]
    ssm_sample = np.concatenate([R[i]["ssm_s"] for i in range(NCORES)], axis=1).reshape(2, 32, 32, 64, 128)
    conv_sample = np.concatenate([R[i]["conv_s"] for i in range(NCORES)], axis=1).reshape(2, 32, 3, CD)
    kvs = [cat(n).reshape(32, w, 2, 16, 64) for n, w in (("kv128_s", 128), ("kv512_s", 512), ("kv2048_s", 2048))]
    outs = (y_prompt, y_sample, ssm_prompt, conv_prompt, kvp[0], kvp[1], kvp[2],
            ssm_sample, conv_sample, kvs[0], kvs[1], kvs[2])
    return tuple(np.ascontiguousarray(o, dtype=np.float32) for o in outs)
```

```python
import numpy as np
from contextlib import ExitStack
import concourse.bass as bass
import concourse.mybir as mybir
from concourse.bass_utils import run_bass_kernel_spmd

F32 = mybir.dt.float32
BF16 = mybir.dt.bfloat16
AF = mybir.ActivationFunctionType
ALU = mybir.AluOpType
AX = mybir.AxisListType

SEM_LIMIT = 30000
NCORES = 8
SEQ = 2048
D = 1024
DI = 2048
NH = 32
CD = 3072
AIN = 5152
NSAMP = 4
NCHP = 16
NCH = 20
DEPTH = 4
ALPHA = (2.0 * DEPTH) ** 0.25
LN_EPS = 1e-5
RMS_EPS = 1e-5
DILS = (1, 4, 16)
WINS = (128, 512, 2048)
ACCW = 1056


class Buf:
    __slots__ = ("name", "w", "r", "sem", "dcount", "excl", "sem_sw", "dcount_sw")

    def __init__(self, name):
        self.name = name
        self.excl = False
        self.sem_sw = None
        self.dcount_sw = 0
        self.w = {}
        self.r = {}
        self.sem = None
        self.dcount = 0


class Q:
    def __init__(self, k, name, self_sync):
        self.k = k
        self.name = name
        self.self_sync = self_sync
        self.ops = []
        self.sem = None
        self.count = 0
        self.seen = {}
        self.sems_used = []

    def _newsem(self):
        self.sem = self.k.new_sem(f"q_{self.name}_{len(self.k.sems)}")
        self.sems_used.append(self.sem)
        self.count = 0


class K:
    def __init__(self, nc, stack):
        self.nc = nc
        self.stack = stack
        self.sems = []
        self.pe = Q(self, "pe", False)
        self.act = Q(self, "act", True)
        self.dve = Q(self, "dve", True)
        self.pool = Q(self, "pool", True)
        self.sp = Q(self, "sp", True)
        self.queues = [self.pe, self.act, self.dve, self.pool, self.sp]
        self.bufs = []
        self.named = {}
        self.final = {}
        self.nobar = set()

    def new_sem(self, name):
        s = self.stack.enter_context(self.nc.semaphore(name))
        self.sems.append(s)
        return s

    def buf(self, name="b"):
        if name not in self.named:
            self.named[name] = Buf(name)
            self.bufs.append(self.named[name])
        return self.named[name]

    def sb(self, name, shape, dtype):
        return self.stack.enter_context(self.nc.sbuf_tensor(name, list(shape), dtype))

    def ps(self, name, shape, dtype=F32):
        return self.stack.enter_context(self.nc.psum_tensor(name, list(shape), dtype))

    def _deps(self, q, reads, writes):
        deps = {}
        for b in reads:
            for d in ((b.w, b.r) if b.excl else (b.w,)):
                for sid, (s, v) in d.items():
                    if deps.get(sid, (None, -1))[1] < v:
                        deps[sid] = (s, v)
        for b in writes:
            for d in (b.w, b.r):
                for sid, (s, v) in d.items():
                    if deps.get(sid, (None, -1))[1] < v:
                        deps[sid] = (s, v)
        waits = []
        for sid, (s, v) in deps.items():
            if (not q.self_sync) and q.sem is not None and sid == id(q.sem):
                continue
            if q.seen.get(sid, -1) >= v:
                continue
            q.seen[sid] = v
            waits.append((s, v))
        return waits

    @staticmethod
    def _mark(bufs_r, bufs_w, sem, val):
        sid = id(sem)
        for b in bufs_r:
            if b.r.get(sid, (None, -1))[1] < val:
                b.r[sid] = (sem, val)
        for b in bufs_w:
            if b.w.get(sid, (None, -1))[1] < val:
                b.w[sid] = (sem, val)

    def op(self, q, fn, reads=(), writes=()):
        if q.sem is None or q.count >= SEM_LIMIT:
            q._newsem()
        waits = self._deps(q, reads, writes)
        q.count += 1
        sem, val = q.sem, q.count
        self.final[id(sem)] = (sem, val)

        def emit(e, fn=fn, waits=waits, sem=sem):
            for (s, v) in waits:
                e.wait_ge(s, v)
            fn(e).then_inc(sem, 1)
        q.ops.append(emit)
        self._mark(reads, writes, sem, val)

    def dma(self, q, out, in_, reads, writes, owner, **kw):
        if q is self.pool:
            if owner.sem_sw is None:
                owner.sem_sw = self.new_sem(f"dsw_{owner.name}")
            owner.dcount_sw += 16
            sem, val = owner.sem_sw, owner.dcount_sw
        else:
            if owner.sem is None:
                owner.sem = self.new_sem(f"d_{owner.name}")
            owner.dcount += 16
            sem, val = owner.sem, owner.dcount
        waits = self._deps(q, reads, writes)
        self.final[id(sem)] = (sem, val)

        def emit(e, waits=waits, sem=sem, out=out, in_=in_, kw=kw):
            for (s, v) in waits:
                e.wait_ge(s, v)
            e.dma_start(out=out, in_=in_, **kw).then_inc(sem, 16)
        q.ops.append(emit)
        self._mark(reads, writes, sem, val)

    def barrier(self, qs=None, final=False):
        for q in (qs or self.queues):
            waits = []
            for sid, (s, v) in self.final.items():
                if (not final) and sid in self.nobar:
                    continue
                if q.seen.get(sid, -1) >= v:
                    continue
                if (not q.self_sync) and q.sem is not None and sid == id(q.sem):
                    continue
                q.seen[sid] = v
                waits.append((s, v))

            def emit(e, waits=waits):
                for (s, v) in waits:
                    e.wait_ge(s, v)
            q.ops.append(emit)

    def emit_all(self):
        nc = self.nc
        with nc.Block() as block:
            @block.tensor
            def _(e):
                for f in self.pe.ops:
                    f(e)

            @block.scalar
            def _(e):
                for f in self.act.ops:
                    f(e)

            @block.vector
            def _(e):
                for f in self.dve.ops:
                    f(e)

            @block.gpsimd
            def _(e):
                for f in self.pool.ops:
                    f(e)

            @block.sync
            def _(e):
                for f in self.sp.ops:
                    f(e)

    def mm(self, out, lhsT, rhs, start, stop, R, W):
        self.op(self.pe, lambda e: e.matmul(out, lhsT=lhsT, rhs=rhs, start=start, stop=stop), R, W)

    def tr(self, out, in_, ident, R, W):
        self.op(self.pe, lambda e: e.transpose(out, in_, ident), R, W)

    def actf(self, out, in_, func, R, W, bias=None, scale=None, accum=None):
        kw = {}
        if bias is not None:
            kw["bias"] = bias
        if scale is not None:
            kw["scale"] = scale
        if accum is not None:
            kw["accum_out"] = accum
        self.op(self.act, lambda e: e.activation(out=out, in_=in_, func=func, **kw), R, W)

    def tt(self, q, out, in0, in1, op, R, W):
        self.op(q, lambda e: e.tensor_tensor(out=out, in0=in0, in1=in1, op=op), R, W)

    def ts(self, q, out, in0, s1, s2, op0, op1, R, W):
        if s2 is None:
            self.op(q, lambda e: e.tensor_scalar(out=out, in0=in0, scalar1=s1, scalar2=None, op0=op0), R, W)
        else:
            self.op(q, lambda e: e.tensor_scalar(out=out, in0=in0, scalar1=s1, scalar2=s2, op0=op0, op1=op1), R, W)

    def stt(self, q, out, in0, scalar, in1, op0, op1, R, W):
        self.op(q, lambda e: e.scalar_tensor_tensor(out=out, in0=in0, scalar=scalar, in1=in1, op0=op0, op1=op1), R, W)

    def cp(self, q, out, in_, R, W):
        if q is self.act:
            self.op(q, lambda e: e.activation(out=out, in_=in_, func=AF.Copy), R, W)
        else:
            self.op(q, lambda e: e.tensor_copy(out=out, in_=in_), R, W)

    def ms(self, q, ap, val, W):
        self.op(q, lambda e: e.memset(ap, val), [], W)

    def red(self, out, in_, op, R, W):
        self.op(self.dve, lambda e: e.tensor_reduce(out=out, in_=in_, axis=AX.X, op=op), R, W)

    def recip(self, out, in_, R, W):
        self.op(self.dve, lambda e: e.reciprocal(out=out, in_=in_), R, W)


class Arena:
    def __init__(self, k, nbytes):
        self.n4 = nbytes // 4
        self.t = k.sb("arena", [128, self.n4], F32)
        self.off = 0

    def reset(self):
        self.off = 0

    def alloc(self, free_shape, dtype, parts=128):
        n = int(np.prod(free_shape))
        esz = 4 if dtype == F32 else 2
        nb4 = (n * esz + 3) // 4
        nb4 = (nb4 + 7) // 8 * 8
        assert self.off + nb4 <= self.n4, f"arena overflow {self.off}+{nb4}>{self.n4}"
        ap = self.t[0:parts, self.off:self.off + nb4]
        self.off += nb4
        if dtype != F32:
            ap = ap.bitcast(dtype)
        ap = ap[:, 0:n]
        if len(free_shape) == 2:
            ap = ap.rearrange("p (a b) -> p a b", a=free_shape[0])
        elif len(free_shape) == 3:
            ap = ap.rearrange("p (a b c) -> p a b c", a=free_shape[0], b=free_shape[1])
        return ap


class Prog:
    def __init__(self, phases=("A0", "A1", "KV", "B0", "B1", "CACHE"), debug=False):
        self.phases = phases
        self.debug = debug
        self.nc = nc = bass.Bass("TRN2", target_bir_lowering=False)
        self.stack = ExitStack()
        self.k = K(nc, self.stack)

        def din(name, shape):
            return nc.dram_tensor(name, list(shape), F32, kind="ExternalInput").ap()

        def dout(name, shape):
            return nc.dram_tensor(name, list(shape), F32, kind="ExternalOutput").ap()

        def dscr(name, shape, dt):
            kind = "ExternalOutput" if debug else "Internal"
            return nc.dram_tensor(name, list(shape), dt, kind=kind).ap()

        self.xp = din("xp", [SEQ, D])
        self.xs = din("xs", [NSAMP, D])
        self.st_ssm = din("st_ssm", [2, NSAMP, DI, 128])
        self.st_conv = din("st_conv", [2, NSAMP, 3, CD])
        self.ck = [din("ck128", [NSAMP, 128, 2048]), din("ck512", [NSAMP, 512, 2048]),
                   din("ck2048", [NSAMP, 2048, 2048])]
        self.a_in = din("a_in", [2, D, AIN])
        self.a_cw = din("a_cw", [2, 4, CD])
        self.a_cb = din("a_cb", [2, CD])
        self.a_dtb = din("a_dtb", [2, NH])
        self.a_log = din("a_log", [2, NH])
        self.a_d = din("a_d", [2, NH])
        self.a_nw = din("a_nw", [2, DI])
        self.a_out = din("a_out", [2, DI, D])
        self.kvw = din("kvw", [D, 6144])
        self.b_in = din("b_in", [2, D, 4096])
        self.b_out = din("b_out", [2, D, D])
        self.ln_g = din("ln_g", [4, D])
        self.ln_b = din("ln_b", [4, D])
        self.yp = dout("yp", [SEQ, D])
        self.ys = dout("ys", [NSAMP, D])
        self.ssm_p = dout("ssm_p", [2, DI, 128])
        self.conv_p = dout("conv_p", [2, 3, CD])
        self.kvp = [dout("kv128_p", [128, 2048]), dout("kv512_p", [512, 2048]), dout("kv2048_p", [2048, 2048])]
        self.ssm_s = dout("ssm_s", [2, NSAMP, DI, 128])
        self.conv_s = dout("conv_s", [2, NSAMP, 3, CD])
        self.kvs = [dout("kv128_s", [NSAMP, 128, 2048]), dout("kv512_s", [NSAMP, 512, 2048]),
                    dout("kv2048_s", [NSAMP, 2048, 2048])]
        self.S = dscr("S", [NCH * 128, D], F32)
        self.G = dscr("G", [NCH * 128, DI], BF16)
        self.KTs = dscr("KTs", [3, D, SEQ], BF16)
        self.Vs = dscr("Vs", [3, SEQ, D], BF16)
        self.ACC = dscr("ACC", [3, SEQ + 128, ACCW], F32)
        self.KVS = dscr("KVS", [NSAMP, 6144], F32)

        k = self.k
        self.bext = k.buf("ext")
        self.bS = [k.buf(f"S{i}") for i in range(NCH)]
        self.bG = [k.buf(f"G{i}") for i in range(NCH)]
        self.consts()
        self.arena = Arena(k, 197 * 1024)
        self.build()
        k.barrier(final=True)
        k.emit_all()
        self.stack.close()

    def consts(self):
        k = self.k
        self.bconst = bc = k.buf("const")
        self.idf = k.sb("idf", [128, 128], F32)
        self.idb = k.sb("idb", [128, 128], BF16)
        self.utri = k.sb("utri", [128, 128], BF16)
        self.ltri = k.sb("ltri", [128, 128], BF16)
        self.causT = k.sb("causT", [128, 128], F32)
        self.onesb = k.sb("onesb", [128, 128], BF16)
        self.onehot0 = k.sb("onehot0", [128, 1], F32)
        tmpf = k.sb("tmpf", [128, 128], F32)
        P = k.pool
        k.ms(P, self.idf[:, :], 0.0, [bc])
        k.op(P, lambda e: e.affine_select(out=self.idf[:, :], in_=self.idf[:, :], pattern=[[-1, 128]],
                                          compare_op=ALU.not_equal, fill=1.0, base=0, channel_multiplier=1), [bc], [bc])
        k.cp(P, self.idb[:, :], self.idf[:, :], [bc], [bc])
        k.ms(P, tmpf[:, :], 1.0, [bc])
        k.op(P, lambda e: e.affine_select(out=tmpf[:, :], in_=tmpf[:, :], pattern=[[1, 128]],
                                          compare_op=ALU.is_ge, fill=0.0, base=0, channel_multiplier=-1), [bc], [bc])
        k.cp(P, self.utri[:, :], tmpf[:, :], [bc], [bc])
        k.cp(P, self.causT[:, :], tmpf[:, :], [bc], [bc])
        k.ms(P, tmpf[:, :], 1.0, [bc])
        k.op(P, lambda e: e.affine_select(out=tmpf[:, :], in_=tmpf[:, :], pattern=[[-1, 128]],
                                          compare_op=ALU.is_gt, fill=0.0, base=0, channel_multiplier=1), [bc], [bc])
        k.cp(P, self.ltri[:, :], tmpf[:, :], [bc], [bc])
        k.ms(P, tmpf[:, :], 1.0, [bc])
        k.cp(P, self.onesb[:, :], tmpf[:, :], [bc], [bc])
        k.cp(P, self.onehot0[:, :], self.idf[:, 0:1], [bc], [bc])
        self.PY = k.ps("PY", [128, 2048])
        self.PA = k.ps("PA", [128, 1024])
        self.PB = k.ps("PB", [128, 512])
        self.PC = k.ps("PC", [128, 512])
        self.bPY = [k.buf(f"PY{i}") for i in range(4)]
        self.bPA = [k.buf(f"PA{i}") for i in range(2)]
        self.bPB = [k.buf("PB0")] * 2
        self.bPC = [k.buf("PC0")] * 2
        for b in self.bPY + self.bPA + self.bPB + self.bPC:
            b.excl = True
        self.hTall = None
        self.bhT = [k.buf(f"hT{i}") for i in range(NCHP)]
        self.hsT = k.sb("hsT", [128, 8, NSAMP], BF16)
        self.bhsT = k.buf("hsT")
        self.knewT = k.sb("knewT", [128, 24, NSAMP], BF16)
        self.bknew = k.buf("knewT")
        self.bKT = [k.buf(f"KTs{g}") for g in range(3)]
        self.bVs = [k.buf(f"Vs{g}") for g in range(3)]
        self.bACC = [k.buf(f"ACC{g}") for g in range(3)]
        self.bKVS = k.buf("KVS")
        self.lng = k.sb("lng", [128, D], F32)
        self.lnb = k.sb("lnb", [128, D], F32)
        self.bln = k.buf("ln")

    def load_ln(self, layer):
        k = self.k
        k.dma(k.sp, self.lng[:, :], self.ln_g[layer:layer + 1, :].partition_broadcast(128), [self.bext], [self.bln], self.bln)
        k.dma(k.sp, self.lnb[:, :], self.ln_b[layer:layer + 1, :].partition_broadcast(128), [self.bext], [self.bln], self.bln)

    def resid_ln(self, q_ps, ps_bufs, hin, bhin, r, br, stat, bstat, junk, bjunk, np_=128):
        k = self.k
        V = k.dve
        k.stt(V, r, hin, ALPHA, q_ps, ALU.mult, ALU.add, [bhin] + ps_bufs, [br])
        k.red(stat[:, 0:1], r, ALU.add, [br], [bstat])
        k.ms(k.pool, stat[:, 1:2], 0.0, [bstat])
        k.actf(junk, r, AF.Square, [br, bstat], [bjunk, bstat], accum=stat[:, 1:2])
        k.ts(V, stat[:, 2:3], stat[:, 0:1], 1.0 / D, None, ALU.mult, None, [bstat], [bstat])
        k.tt(V, stat[:, 3:4], stat[:, 2:3], stat[:, 2:3], ALU.mult, [bstat], [bstat])
        k.stt(V, stat[:, 4:5], stat[:, 1:2], 1.0 / D, stat[:, 3:4], ALU.mult, ALU.subtract, [bstat], [bstat])
        k.ts(V, stat[:, 4:5], stat[:, 4:5], LN_EPS, None, ALU.add, None, [bstat], [bstat])
        k.actf(stat[:, 5:6], stat[:, 4:5], AF.Ln, [bstat], [bstat])
        k.actf(stat[:, 6:7], stat[:, 5:6], AF.Exp, [bstat], [bstat], scale=-0.5)
        k.ts(V, r, r, stat[:, 2:3], stat[:, 6:7], ALU.subtract, ALU.mult, [br, bstat], [br])
        k.tt(V, r, r, self.lng[0:np_, :], ALU.mult, [br, self.bln], [br])
        k.tt(V, r, r, self.lnb[0:np_, :], ALU.add, [br, self.bln], [br])

    def build(self):
        for L in range(2):
            if f"A{L}" in self.phases:
                self.a_sweep1(L)
                self.k.barrier()
                self.a_sweep2(L)
                self.k.barrier()
        if "KV" in self.phases:
            self.kv_phase()
            self.k.barrier()
        for j in range(2):
            if f"B{j}" in self.phases:
                self.b_phase(j)
                self.k.barrier()

    def tok_ap(self, g, kc, pos0, cnt):
        dil = DILS[g]
        n = SEQ // dil
        hv = self.hTall[:, kc, :]
        if dil == 1:
            return hv[:, pos0:pos0 + cnt]
        hv = hv.rearrange("p (i r) -> p r i", r=dil)
        r0, i0 = pos0 // n, pos0 % n
        if cnt <= n - i0:
            return hv[:, r0, i0:i0 + cnt]
        assert i0 == 0 and cnt % n == 0
        return hv[:, r0:r0 + cnt // n, :]

    def row_ap(self, dram2d, g, pos0):
        dil = DILS[g]
        n = SEQ // dil
        r0, i0 = pos0 // n, pos0 % n
        if dil == 1:
            return dram2d[pos0:pos0 + 128, :]
        return dram2d.rearrange("(i r) c -> r i c", r=dil)[r0, i0:i0 + 128, :]

    def cache_copy(self):
        k = self.k
        bcc = k.buf("cachecopy")
        for g in range(3):
            W = WINS[g]
            for b in range(NSAMP):
                nsplit = max(1, W // 512)
                rows = (W - 1)
                step = (rows + nsplit - 1) // nsplit
                for r0 in range(0, rows, step):
                    r1 = min(rows, r0 + step)
                    k.dma(k.act, self.kvs[g][b, r0:r1, :], self.ck[g][b, r0 + 1:r1 + 1, :], [self.bext], [self.bext], bcc)
        k.nobar.add(id(bcc.sem))

    def a_sweep1(self, L):
        k = self.k
        A = self.arena
        A.reset()
        PE, ACT, V, PL, SP = k.pe, k.act, k.dve, k.pool, k.sp
        bext, bc = self.bext, self.bconst
        PY, PA, PB, PC = self.PY, self.PA, self.PB, self.PC
        bPY, bPA, bPB, bPC = self.bPY, self.bPA, self.bPB, self.bPC
        PAb = PA[:, :].bitcast(BF16)
        PYb = PY[:, :].bitcast(BF16)
        PCb = PC[:, :].bitcast(BF16)
        idf, idb = self.idf, self.idb

        Win = A.alloc((8, AIN), BF16); bWin = k.buf("Win")
        dg = [A.alloc((4, 128), BF16) for _ in range(2)]; bdg = [k.buf(f"dg{i}") for i in range(2)]
        diagD = A.alloc((16, 128), BF16); bdiagD = k.buf("diagD")
        cw = A.alloc((24, 4), F32); cbias = A.alloc((24,), F32); dcol = A.alloc((16,), F32); bcw = k.buf("cw")
        dtb = A.alloc((NH,), F32); abc = A.alloc((NH,), F32); bsm = k.buf("smallw")
        hin = [A.alloc((D,), F32) for _ in range(2)]; bhin = [k.buf(f"hin{i}") for i in range(2)]
        hT = A.alloc((8, 256), BF16); bhT = [k.buf(f"hTl{i}") for i in range(2)]
        xpre = A.alloc((24, 259), BF16); bxpre = k.buf("xpre")
        xbcT = A.alloc((24, 256), BF16); bxbc = k.buf("xbcT")
        sz = A.alloc((DI,), F32); bsz = k.buf("sz"); bszg = [k.buf(f"szg{i}") for i in range(4)]
        xdt = A.alloc((DI,), BF16); bxdt = k.buf("xdt")
        xdts = A.alloc((DI,), BF16); bxdts = k.buf("xdts")
        Btok = A.alloc((512,), BF16); bBtok = k.buf("Btok")
        Rt = [A.alloc((8, 128), BF16) for _ in range(2)]; bR = [k.buf(f"R{i}") for i in range(2)]
        E = [A.alloc((8, 128), BF16) for _ in range(2)]; bE = [k.buf(f"E{i}") for i in range(2)]
        Mg = A.alloc((NH, 128), BF16); bMg = [k.buf(f"Mg{i}") for i in range(4)]
        cbm = [A.alloc((128,), BF16) for _ in range(2)]; bcbm = [k.buf(f"cbm{i}") for i in range(2)]
        yo = A.alloc((512,), BF16); byo = k.buf("yo")
        gn = A.alloc((DI,), BF16); bgn = k.buf("gn")
        junk = yo; bjunk = byo
        hs = A.alloc((DI,), F32); bhs = [k.buf(f"hs{i}") for i in range(4)]
        hb = A.alloc((DI,), BF16); bhb = [k.buf(f"hb{i}") for i in range(4)]
        hso = sz.rearrange("p (a b) -> p a b", a=16); bhso = bsz
        sm = A.alloc((512,), F32); bsmt = k.buf("sm")
        dAhl = A.alloc((2, NH), BF16); bdA = k.buf("dAhl")
        last3 = A.alloc((24, 3), F32); blast3 = k.buf("last3")
        newrow = A.alloc((24, 2), F32); bnewrow = k.buf("newrow")
        ss = A.alloc((16,), F32); bss = k.buf("ss")
        craw = A.alloc((3, 128), F32, parts=24); bcraw = k.buf("craw")
        rowst = A.alloc((640,), F32, parts=24); browst = k.buf("rowst")
        dt_raw, dt_abs, dt_, dA, acs, alast, ee, cd, te, dts = [sm[:, i * 32:(i + 1) * 32] for i in range(10)]

        if not (L == 1 and getattr(self, "win_prefetched", False)):
            for kc in range(8):
                k.dma(PL, Win[:, kc, :], self.a_in[L, kc * 128:(kc + 1) * 128, :], [bext], [bWin], bWin)
        cwraw = A.alloc((4, 128), F32, parts=24); cbraw = A.alloc((128,), F32, parts=24); braw = k.buf("raw")
        k.dma(SP, cwraw, self.a_cw[L].rearrange("t (c p) -> c t p", p=128), [bext], [braw], braw)
        k.dma(SP, cbraw, self.a_cb[L].rearrange("(c p) -> c p", p=128), [bext], [braw], braw)
        for t in range(4):
            k.tr(PB[:, t * 24:(t + 1) * 24], cwraw[:, t, :], idf[0:24, 0:24], [braw, bc], [bPB[0]])
        k.cp(V, cw, PB[:, 0:96].rearrange("p (t c) -> p c t", t=4), [bPB[0]], [bcw])
        k.tr(PB[:, 256:280], cbraw, idf[0:24, 0:24], [braw, bc], [bPB[1]])
        k.cp(V, cbias, PB[:, 256:280], [bPB[1]], [bcw])
        dfull = A.alloc((NH,), F32)
        k.dma(SP, dfull, self.a_d[L:L + 1, :].partition_broadcast(128), [bext], [bsm], bsm)
        for two in range(2):
            k.cp(V, dcol[two * 64:(two + 1) * 64, :], dfull[two * 64:(two + 1) * 64, :].rearrange("p (c two) -> p two c", two=2)[:, two, :],
                 [bsm], [bcw])
        k.dma(SP, dtb, self.a_dtb[L:L + 1, :].partition_broadcast(128), [bext], [bsm], bsm)
        k.dma(SP, abc, self.a_log[L:L + 1, :].partition_broadcast(128), [bext], [bsm], bsm)
        k.actf(abc, abc, AF.Exp, [bsm], [bsm])
        k.ts(V, abc, abc, -1.0, None, ALU.mult, None, [bsm], [bsm])
        for fc in range(16):
            k.ts(PL, diagD[:, fc, :], idb[:, :], dcol[:, fc:fc + 1], None, ALU.mult, None, [bc, bcw], [bdiagD])
        k.ms(PL, ss, 0.0, [bss])

        def state_out(dst):
            for half in range(2):
                for bl in range(8):
                    blk = half * 8 + bl
                    k.tr(PA[:, bl * 128:(bl + 1) * 128], hs[:, blk * 128:(blk + 1) * 128], idf[:, :],
                         [bhs[blk // 4], bc], [bPA[bl // 4]])
                k.cp(V if half else ACT, hso[:, half * 8:(half + 1) * 8, :],
                     PA[:, :].rearrange("p (a b) -> p a b", a=8), bPA, [bhso] + bszg)
            k.dma(PL, dst.rearrange("(a p) n -> p a n", p=128), hso, [bhso] + bszg, [bext], bhso)

        nsc = NCH // 2

        def load_hin(c, j):
            if L == 0:
                if c < NCHP:
                    k.dma(SP, hin[j], self.xp[c * 128:(c + 1) * 128, :], [bext], [bhin[j]], bhin[j])
                else:
                    b = c - NCHP
                    k.ms(PL, hin[j], 0.0, [bhin[j]])
                    k.dma(SP, hin[j][0:1, :], self.xs[b:b + 1, :], [bext], [bhin[j]], bhin[j])
                    k.dma(PL, self.S[c * 128:(c + 1) * 128, :], hin[j], [bhin[j]], [self.bS[c]], bhin[j])
            else:
                k.dma(SP, hin[j], self.S[c * 128:(c + 1) * 128, :], [self.bS[c]], [bhin[j]], bhin[j])

        for j in range(2):
            load_hin(j, j)
        for sc in range(nsc):
            samp = sc >= NCHP // 2
            if sc == 1 and L == 1 and "CACHE" in self.phases:
                self.cache_copy()
            for j in range(2):
                c = 2 * sc + j
                for kc in range(8):
                    k.tr(PA[:, kc * 128:(kc + 1) * 128], hin[j][:, kc * 128:(kc + 1) * 128], idf[:, :],
                         [bhin[j], bc], [bPA[kc // 4]])
                k.cp(ACT, hT[:, :, j * 128:(j + 1) * 128], PA[:, :].rearrange("p (a b) -> p a b", a=8), bPA, [bhT[j]])
                if sc + 1 < nsc:
                    load_hin(2 * (sc + 1) + j, j)
            if sc == 0 or samp:
                k.ms(PL, xpre[:, :, 0:3], 0.0, [bxpre])
            else:
                k.cp(V, xpre[:, :, 0:3], xpre[:, :, 256:259], [bxpre], [bxpre])
            for fc in range(24):
                ps, bps = (PB[:, 0:256], bPB[0]) if fc % 2 == 0 else (PC[:, 0:256], bPC[0])
                for kc in range(8):
                    k.mm(ps, Win[:, kc, 2048 + fc * 128:2048 + (fc + 1) * 128], hT[:, kc, :], kc == 0, kc == 7,
                         [bWin] + bhT, [bps])
                k.cp(V if fc % 2 == 0 else ACT, xpre[:, fc, 3:259], ps, [bps], [bxpre])
                if sc == NCHP // 2 - 1:
                    k.cp(V, last3[:, fc, :], ps[:, 253:256], [bps], [blast3])
                if samp:
                    k.cp(V, newrow[:, fc, :], ps.rearrange("p (j t) -> p j t", j=2)[:, :, 0], [bps], [bnewrow])
            if sc == NCHP // 2 - 1:
                for rr in range(3):
                    k.tr(PC[0:24, 128 * rr:128 * (rr + 1)], last3[:, :, rr], idf[:, :], [blast3, bc], [bPC[0]])
                k.cp(V, rowst[:, 0:384], PC[0:24, 0:384], [bPC[0]], [browst])
                k.dma(PL, self.conv_p[L].rearrange("r (c p) -> c r p", p=128), rowst[:, 0:384].rearrange("c (r p) -> c r p", r=3),
                      [browst], [bext], browst)
            if samp:
                for j in range(2):
                    b = 2 * sc + j - NCHP
                    k.dma(SP, craw, self.st_conv[L, b].rearrange("r (c p) -> c r p", p=128), [bext], [bcraw], bcraw)
                    for rr in range(3):
                        k.tr(PC[:, rr * 24:(rr + 1) * 24], craw[:, rr, :], idf[0:24, 0:24], [bcraw, bc], [bPC[0]])
                    k.cp(V, xpre[:, :, j * 128:j * 128 + 3], PC[:, 0:72].rearrange("p (r c) -> p c r", r=3), [bPC[0]], [bxpre])
                    k.tr(PC[0:24, 128:256], newrow[:, :, j], idf[:, :], [bnewrow, bc], [bPC[0]])
                    k.cp(V, rowst[:, 384 + 128 * j:512 + 128 * j], PC[0:24, 128:256], [bPC[0]], [browst])
                    k.dma(PL, self.conv_s[L, b, 2, :].rearrange("(c p) -> c p", p=128), rowst[:, 384 + 128 * j:512 + 128 * j],
                          [browst], [bext], browst)
                    k.dma(SP, self.conv_s[L, b, 0:2, :], self.st_conv[L, b, 1:3, :], [bext], [bext], browst)
            for fc in range(24):
                ps, bps = (PB[:, 0:256], bPB[0]) if fc % 2 == 0 else (PC[:, 0:256], bPC[0])
                dgi, bdgi = dg[fc % 2], bdg[fc % 2]
                k.tt(V, dgi, idb[:, :].unsqueeze(1).to_broadcast([128, 4, 128]),
                     cw[:, fc, :].unsqueeze(2).to_broadcast([128, 4, 128]), ALU.mult, [bc, bcw], [bdgi])
                for t in range(4):
                    k.mm(ps, dgi[:, t, :], xpre[:, fc, t:t + 256], t == 0, t == 3, [bdgi, bxpre], [bps])
                k.actf(xbcT[:, fc, :], ps, AF.Silu, [bps, bcw], [bxbc], bias=cbias[:, fc:fc + 1])

            for j in range(2):
                c = 2 * sc + j
                t0 = j * 128
                if c == 0:
                    for g in range(4):
                        k.ms(PL, hs[:, g * 512:(g + 1) * 512], 0.0, [bhs[g]])
                        k.ms(PL, hb[:, g * 512:(g + 1) * 512], 0.0, [bhb[g]])
                if samp:
                    b = c - NCHP
                    k.dma(SP, hso, self.st_ssm[L, b].rearrange("(a p) n -> p a n", p=128), [bext], [bhso] + bszg, bhso)
                    for half in range(2):
                        for bl in range(8):
                            blk = half * 8 + bl
                            k.tr(PA[:, bl * 128:(bl + 1) * 128], hso[:, blk, :], idf[:, :], [bhso, bc] + bszg, [bPA[bl // 4]])
                        for gg in range(2):
                            g = half * 2 + gg
                            k.cp(V, hs[:, g * 512:(g + 1) * 512], PA[:, gg * 512:(gg + 1) * 512], [bPA[gg]], [bhs[g]])
                            k.cp(ACT, hb[:, g * 512:(g + 1) * 512], PA[:, gg * 512:(gg + 1) * 512], [bPA[gg]], [bhb[g]])
                for kc in range(8):
                    k.mm(PC[:, 0:32], hT[:, kc, t0:t0 + 128], Win[:, kc, 5120:5152], kc == 0, kc == 7, [bhT[j], bWin], [bPC[0]])
                k.tt(V, dt_raw, PC[:, 0:32], dtb, ALU.add, [bPC[0], bsm], [bsmt])
                k.actf(dt_abs, dt_raw, AF.Abs, [bsmt], [bsmt])
                k.actf(dt_abs, dt_abs, AF.Exp, [bsmt], [bsmt], scale=-1.0)
                k.actf(dt_abs, dt_abs, AF.Ln, [bsmt], [bsmt], bias=1.0, scale=1.0)
                k.stt(V, dt_, dt_raw, 0.0, dt_abs, ALU.max, ALU.add, [bsmt], [bsmt])
                if samp:
                    k.ts(V, dt_, dt_, self.onehot0[:, 0:1], None, ALU.mult, None, [bsmt, bc], [bsmt])
                k.tt(V, dA, dt_, abc, ALU.mult, [bsmt, bsm], [bsmt])
                k.cp(V, dAhl[:, 0, :], dA, [bsmt], [bdA])
                k.tt(V, dAhl[:, 1, :], dA, dAhl[:, 0, :], ALU.subtract, [bsmt, bdA], [bdA])
                for i, lhs in enumerate((self.utri, self.onesb)):
                    o = PC[:, 32 + 32 * i:64 + 32 * i]
                    k.mm(o, lhs[:, :], dAhl[:, 0, :], True, False, [bc, bdA], [bPC[0]])
                    k.mm(o, lhs[:, :], dAhl[:, 1, :], False, True, [bc, bdA], [bPC[0]])
                k.cp(V, acs, PC[:, 32:64], [bPC[0]], [bsmt])
                k.cp(V, alast, PC[:, 64:96], [bPC[0]], [bsmt])
                k.actf(ee, acs, AF.Exp, [bsmt], [bsmt])
                k.actf(cd, alast, AF.Exp, [bsmt], [bsmt])
                k.tt(V, te, alast, acs, ALU.subtract, [bsmt], [bsmt])
                k.actf(te, te, AF.Exp, [bsmt], [bsmt])
                k.tt(V, dts, dt_, te, ALU.mult, [bsmt], [bsmt])
                for fc in range(16):
                    k.tr(PYb[:, fc * 128:(fc + 1) * 128], xbcT[:, fc, t0:t0 + 128], idb[:, :], [bxbc, bc], [bPY[fc // 8]])
                xv = PYb[:, 0:2048].rearrange("p (h d) -> p h d", h=NH)
                k.tt(V, xdt.rearrange("p (h d) -> p h d", h=NH), xv, dt_.unsqueeze(2).to_broadcast([128, NH, 64]),
                     ALU.mult, bPY[0:2] + [bsmt], [bxdt])
                k.tt(V, xdts.rearrange("p (h d) -> p h d", h=NH), xv, dts.unsqueeze(2).to_broadcast([128, NH, 64]),
                     ALU.mult, bPY[0:2] + [bsmt], [bxdts])
                for g in range(4):
                    k.tr(PCb[:, 512 + g * 128:512 + (g + 1) * 128], xbcT[:, 16 + g, t0:t0 + 128], idb[:, :], [bxbc, bc], [bPC[1]])
                k.cp(ACT, Btok, PCb[:, 512:1024], [bPC[1]], [bBtok])
                for half in range(2):
                    for q in range(2):
                        col = (2 * half + q) * 512
                        for kc in range(8):
                            k.mm(PA[:, q * 512:(q + 1) * 512], hT[:, kc, t0:t0 + 128], Win[:, kc, col:col + 512],
                                 kc == 0, kc == 7, [bhT[j], bWin], [bPA[q]])
                    k.actf(sz[:, half * 1024:(half + 1) * 1024], PA[:, :], AF.Silu, bPA, [bsz, bszg[2 * half], bszg[2 * half + 1]])
                for g in range(4):
                    hsg = hs[:, g * 512:(g + 1) * 512]
                    k.tt(V, hsg.rearrange("p (h d) -> p h d", h=8), hsg.rearrange("p (h d) -> p h d", h=8),
                         cd[:, g * 8:(g + 1) * 8].unsqueeze(2).to_broadcast([128, 8, 64]), ALU.mult, [bhs[g], bsmt], [bhs[g]])

                def rt(g):
                    i = g % 2
                    k.tt(V, Rt[i], self.utri[:, :].unsqueeze(1).to_broadcast([128, 8, 128]),
                         dAhl[:, 0, g * 8:(g + 1) * 8].unsqueeze(2).to_broadcast([128, 8, 128]), ALU.mult, [bc, bdA], [bR[i]])

                rt(0)
                for g in range(4):
                    i = g % 2
                    sps, bsp = (PA, bPA) if i == 0 else (PY[:, 1024:2048], bPY[2:4])
                    for half in range(2):
                        o = sps[:, half * 512:(half + 1) * 512]
                        k.mm(o, self.ltri[:, :], Rt[i][:, half * 4:(half + 1) * 4, :], True, True, [bc, bR[i]], [bsp[half]])
                    cps, bcp = (PB[:, 0:128], bPB[0]) if i == 0 else (PC[:, 128:256], bPC[0])
                    k.mm(cps, xbcT[:, 16 + g, t0:t0 + 128], xbcT[:, 20 + g, t0:t0 + 128], True, True, [bxbc], [bcp])
                    if g + 1 < 4:
                        rt(g + 1)
                    k.actf(E[i], sps[:, 0:1024].rearrange("p (a b) -> p a b", a=8), AF.Exp, list(bsp), [bE[i]])
                    k.tt(V, cbm[i], cps, self.causT[:, :], ALU.mult, [bcp, bc], [bcbm[i]])
                    k.tt(V, Mg[:, g * 8:(g + 1) * 8, :], E[i], cbm[i].unsqueeze(1).to_broadcast([128, 8, 128]), ALU.mult,
                         [bE[i], bcbm[i]], [bMg[g]])
                for g in range(4):
                    hsg = hs[:, g * 512:(g + 1) * 512]
                    k.mm(PB[:, :], xbcT[:, 20 + g, t0:t0 + 128], hb[:, g * 512:(g + 1) * 512], True, True, [bxbc, bhb[g]], bPB)
                    k.mm(PC[:, :], Btok[:, g * 128:(g + 1) * 128], xdts[:, g * 512:(g + 1) * 512], True, True, [bBtok, bxdts], bPC)
                    k.tt(V, yo.rearrange("p (h d) -> p h d", h=8), PB[:, :].rearrange("p (h d) -> p h d", h=8),
                         ee[:, g * 8:(g + 1) * 8].unsqueeze(2).to_broadcast([128, 8, 64]), ALU.mult, bPB + [bsmt], [byo])
                    for pr in range(4):
                        fc = g * 4 + pr
                        k.mm(PY[:, fc * 128:(fc + 1) * 128], xbcT[:, fc, t0:t0 + 128], diagD[:, fc, :], pr == 0, False,
                             [bxbc, bdiagD], [bPY[g]])
                    for hh in range(8):
                        h = g * 8 + hh
                        k.mm(PY[:, h * 64:(h + 1) * 64], Mg[:, h, :], xdt[:, h * 64:(h + 1) * 64], False, False,
                             [bMg[g], bxdt], [bPY[g]])
                    k.mm(PY[:, g * 512:(g + 1) * 512], idb[:, :], yo, False, True, [bc, byo], [bPY[g]])
                    k.tt(V, hsg, hsg, PC[:, :], ALU.add, [bhs[g]] + bPC, [bhs[g]])
                    k.cp(ACT, hb[:, g * 512:(g + 1) * 512], hsg, [bhs[g]], [bhb[g]])
                    szg = sz[:, g * 512:(g + 1) * 512]
                    if g == 0:
                        k.ms(PL, ss[:, 0:4], 0.0, [bss])
                    k.tt(V, szg, PY[:, g * 512:(g + 1) * 512], szg, ALU.mult, [bPY[g], bszg[g]], [bszg[g]])
                    k.actf(junk, szg, AF.Square, [bszg[g], bss], [bjunk, bss], accum=ss[:, g:g + 1])
                k.ts(V, ss[:, 4:8], ss[:, 0:4], 1.0 / 512, RMS_EPS, ALU.mult, ALU.add, [bss], [bss])
                k.actf(ss[:, 8:12], ss[:, 4:8], AF.Ln, [bss], [bss])
                k.actf(ss[:, 12:16], ss[:, 8:12], AF.Exp, [bss], [bss], scale=-0.5)
                for g in range(4):
                    szg = sz[:, g * 512:(g + 1) * 512]
                    if g % 2 == 0:
                        k.actf(gn[:, g * 512:(g + 1) * 512], szg, AF.Copy, [bszg[g], bss], [bgn], scale=ss[:, 12 + g:13 + g])
                    else:
                        k.ts(V, gn[:, g * 512:(g + 1) * 512], szg, ss[:, 12 + g:13 + g], None, ALU.mult, None, [bszg[g], bss], [bgn])
                k.dma(PL, self.G[c * 128:(c + 1) * 128, :], gn, [bgn], [self.bG[c]], bgn)
                if c == NCHP - 1:
                    state_out(self.ssm_p[L])
                if samp:
                    state_out(self.ssm_s[L, c - NCHP])

    def a_sweep2(self, L):
        k = self.k
        A = self.arena
        A.reset()
        PE, ACT, V, PL, SP = k.pe, k.act, k.dve, k.pool, k.sp
        bext, bc = self.bext, self.bconst
        PY, PA = self.PY, self.PA
        bPY, bPA = self.bPY, self.bPA
        PAb = PA[:, :].bitcast(BF16)
        if L == 1:
            self.hTall = A.alloc((8, SEQ), BF16)
        else:
            WinN = A.alloc((8, AIN), BF16); bWinN = k.buf("Win")
            self.win_prefetched = True
        Wout = A.alloc((16, D), BF16); bWout = k.buf("Wout")
        wst = [A.alloc((D,), F32) for _ in range(2)]; bwst = [k.buf(f"wst{i}") for i in range(2)]
        nw = A.alloc((16,), F32); bnw = k.buf("nw")
        gnt = [A.alloc((DI,), BF16) for _ in range(2)]; bgnt = [k.buf(f"gnt{i}") for i in range(2)]
        gT = [A.alloc((16, 128), BF16) for _ in range(2)]; bgT = [k.buf(f"gT{i}") for i in range(2)]
        hin = [A.alloc((D,), F32) for _ in range(2)]; bhin = [k.buf(f"hin{i}") for i in range(2)]
        r = [A.alloc((D,), F32) for _ in range(2)]; br = [k.buf(f"r{i}") for i in range(2)]
        stat = [A.alloc((8,), F32) for _ in range(2)]; bstat = [k.buf(f"stat{i}") for i in range(2)]
        junk = [A.alloc((D,), BF16) for _ in range(2)]; bjunk = [k.buf(f"junk2{i}") for i in range(2)]
        hTl = A.alloc((8, 128), BF16); bhTl = k.buf("hTl2")
        self.load_ln(L)
        nwraw = A.alloc((128,), F32, parts=16)
        k.dma(SP, nwraw, self.a_nw[L].rearrange("(c p) -> c p", p=128), [bext], [bnw], bnw)
        k.tr(PA[:, 0:16], nwraw, self.idf[0:16, 0:16], [bnw, bc], [bPA[0]])
        k.cp(V, nw, PA[:, 0:16], [bPA[0]], [bnw])
        for fc in range(16):
            i = fc % 2
            k.dma(SP, wst[i], self.a_out[L, fc * 128:(fc + 1) * 128, :], [bext], [bwst[i]], bwst[i])
            k.actf(Wout[:, fc, :], wst[i], AF.Copy, [bwst[i], bnw], [bWout], scale=nw[:, fc:fc + 1])
        for c in range(NCH):
            i = c % 2
            samp = c >= NCHP
            if L == 0 and c % 2 == 0 and c // 2 < 8:
                kc = c // 2
                k.dma(PL, WinN[:, kc, :], self.a_in[1, kc * 128:(kc + 1) * 128, :], [bext], [bWinN], bWinN)
            k.dma(SP, gnt[i], self.G[c * 128:(c + 1) * 128, :], [self.bG[c]], [bgnt[i]], bgnt[i])
            if L == 0 and not samp:
                k.dma(SP, hin[i], self.xp[c * 128:(c + 1) * 128, :], [bext], [bhin[i]], bhin[i])
            else:
                k.dma(SP, hin[i], self.S[c * 128:(c + 1) * 128, :], [self.bS[c]], [bhin[i]], bhin[i])
            for fc in range(16):
                k.tr(PAb[:, fc * 128:(fc + 1) * 128], gnt[i][:, fc * 128:(fc + 1) * 128], self.idb[:, :], [bgnt[i], bc], [bPA[fc // 8]])
            k.cp(ACT, gT[i], PAb.rearrange("p (a b) -> p a b", a=16), bPA, [bgT[i]])
            for half in range(2):
                for fc in range(16):
                    k.mm(PY[:, i * 1024 + half * 512:i * 1024 + (half + 1) * 512], gT[i][:, fc, :], Wout[:, fc, half * 512:(half + 1) * 512],
                         fc == 0, fc == 15, [bgT[i], bWout], [bPY[2 * i + half]])
            self.resid_ln(PY[:, i * 1024:(i + 1) * 1024], bPY[2 * i:2 * i + 2], hin[i], bhin[i], r[i], br[i], stat[i], bstat[i], junk[i], bjunk[i])
            k.dma(PL, self.S[c * 128:(c + 1) * 128, :], r[i], [br[i]], [self.bS[c]], br[i])
            if L == 1:
                for kc in range(8):
                    k.tr(PA[:, kc * 128:(kc + 1) * 128], r[i][:, kc * 128:(kc + 1) * 128], self.idf[:, :], [br[i], bc], [bPA[kc // 4]])
                if not samp:
                    k.cp(ACT, self.hTall[:, :, c * 128:(c + 1) * 128], PA[:, :].rearrange("p (a b) -> p a b", a=8), bPA, [self.bhT[c]])
                else:
                    b = c - NCHP
                    k.cp(ACT, self.hsT[:, :, b:b + 1], PA[:, :].rearrange("p (a b) -> p a b", a=8)[:, :, 0:1], bPA, [self.bhsT])


    def kv_phase(self):
        k = self.k
        A = self.arena
        A.reset()
        PE, ACT, V, PL, SP = k.pe, k.act, k.dve, k.pool, k.sp
        bext, bc = self.bext, self.bconst
        PY, PA, PB, PC = self.PY, self.PA, self.PB, self.PC
        bPY, bPA, bPB, bPC = self.bPY, self.bPA, self.bPB, self.bPC
        self.hTall = A.alloc((8, SEQ), BF16)
        Wkv = A.alloc((8, 6144), BF16); bW = k.buf("Wkv")
        KT = [A.alloc((8, 512), BF16) for _ in range(2)]; bKTt = [k.buf(f"KTt{i}") for i in range(2)]
        kvf = [A.alloc((2048,), F32) for _ in range(2)]; bkvf = [k.buf(f"kvf{i}") for i in range(2)]
        Vb = [A.alloc((D,), BF16) for _ in range(2)]; bVb = [k.buf(f"Vb{i}") for i in range(2)]
        kvs_sb = A.alloc((6144,), F32, parts=NSAMP); bkvs = k.buf("kvs_sb")
        for kc in range(8):
            k.dma(PL, Wkv[:, kc, :], self.kvw[kc * 128:(kc + 1) * 128, :], [bext], [bW], bW)
        allhT = self.bhT
        it = 0
        for g in range(3):
            dil = DILS[g]
            n = SEQ // dil
            W = WINS[g]
            for q4 in range(4):
                pb = q4 * 512
                kt = KT[q4 % 2]; bkt = bKTt[q4 % 2]
                for fc in range(8):
                    ps, bps = (PB[:, :], bPB[0]) if fc % 2 == 0 else (PC[:, :], bPC[0])
                    for kc in range(8):
                        k.mm(ps, Wkv[:, kc, g * 1024 + fc * 128:g * 1024 + (fc + 1) * 128], self.tok_ap(g, kc, pb, 512),
                             kc == 0, kc == 7, [bW] + allhT, [bps])
                    k.cp(V if fc % 2 == 0 else ACT, kt[:, fc, :], ps, [bps], [bkt])
                k.dma(PL, self.KTs[g].rearrange("(c p) t -> p c t", p=128)[:, :, pb:pb + 512], kt, [bkt], [self.bKT[g]], bkt)
                for tt in range(4):
                    pos0 = pb + tt * 128
                    i0 = pos0 % n
                    need_k = (i0 * dil + (dil - 1)) >= SEQ - W
                    i2 = it % 2
                    it += 1
                    for half in range(2):
                        col = 3072 + g * 1024 + half * 512
                        for kc in range(8):
                            k.mm(PA[:, half * 512:(half + 1) * 512], self.tok_ap(g, kc, pos0, 128), Wkv[:, kc, col:col + 512],
                                 kc == 0, kc == 7, [bW] + allhT, [bPA[half]])
                    k.cp(ACT, Vb[i2], PA[:, :], bPA, [bVb[i2]])
                    k.dma(PL, self.Vs[g][pos0:pos0 + 128, :], Vb[i2], [bVb[i2]], [self.bVs[g]], bVb[i2])
                    if need_k:
                        k.cp(V, kvf[i2][:, 1024:2048], PA[:, :], bPA, [bkvf[i2]])
                        for half in range(2):
                            col = g * 1024 + half * 512
                            for kc in range(8):
                                k.mm(PY[:, half * 512:(half + 1) * 512], self.tok_ap(g, kc, pos0, 128), Wkv[:, kc, col:col + 512],
                                     kc == 0, kc == 7, [bW] + allhT, [bPY[half]])
                        k.cp(V, kvf[i2][:, 0:1024], PY[:, 0:1024], bPY[0:2], [bkvf[i2]])
                        r0 = pos0 // n
                        tok0 = i0 * dil + r0
                        if dil == 1:
                            dst = self.kvp[g][tok0 - (SEQ - W):tok0 - (SEQ - W) + 128, :]
                        else:
                            ib = (SEQ - W) // dil
                            dst = self.kvp[g].rearrange("(i r) c -> r i c", r=dil)[r0, i0 - ib:i0 - ib + 128, :]
                        k.dma(PL, dst, kvf[i2], [bkvf[i2]], [bext], bkvf[i2])
        for cg in range(12):
            ps, bps = (PB[0:NSAMP, :], bPB[0]) if cg % 2 == 0 else (PC[0:NSAMP, :], bPC[0])
            for kc in range(8):
                k.mm(ps, self.hsT[:, kc, :], Wkv[:, kc, cg * 512:(cg + 1) * 512], kc == 0, kc == 7, [bW, self.bhsT], [bps])
            k.cp(V if cg % 2 == 0 else ACT, kvs_sb[:, cg * 512:(cg + 1) * 512], ps, [bps], [bkvs])
        k.dma(PL, self.KVS, kvs_sb, [bkvs], [self.bKVS], bkvs)
        for g in range(3):
            W = WINS[g]
            for b in range(NSAMP):
                k.dma(PL, self.kvs[g][b, W - 1:W, 0:1024], kvs_sb[b:b + 1, g * 1024:(g + 1) * 1024], [bkvs], [bext], bkvs)
                k.dma(PL, self.kvs[g][b, W - 1:W, 1024:2048], kvs_sb[b:b + 1, 3072 + g * 1024:3072 + (g + 1) * 1024], [bkvs], [bext], bkvs)
        for c in range(24):
            k.tr(PA[:, c * 4:(c + 1) * 4], kvs_sb[:, c * 128:(c + 1) * 128], self.idf[0:NSAMP, 0:NSAMP], [bkvs, bc], [bPA[0]])
        k.cp(V, self.knewT[:, :, :], PA[:, 0:96].rearrange("p (c b) -> p c b", c=24), [bPA[0]], [self.bknew])

    def b_phase(self, j):
        k = self.k
        A = self.arena
        A.reset()
        layer = 2 + j
        last = (j == 1)
        PE, ACT, V, PL, SP = k.pe, k.act, k.dve, k.pool, k.sp
        bext, bc = self.bext, self.bconst
        PY, PA, PB, PC = self.PY, self.PA, self.PB, self.PC
        bPY, bPA, bPB, bPC = self.bPY, self.bPA, self.bPB, self.bPC
        PYb = PY[:, :].bitcast(BF16)
        PAb = PA[:, :].bitcast(BF16)
        idf, idb = self.idf, self.idb
        self.hTall = A.alloc((8, SEQ), BF16)
        allhT = self.bhT
        Wb = A.alloc((8, 4096), BF16); bWb = k.buf("Wbin")
        Wo = A.alloc((8, D), BF16); bWo = k.buf("Wbout")
        AB = A.alloc((16, 256), F32); bAB = k.buf("AB")
        DM = A.alloc((3, 256), F32); bDM = k.buf("DM")
        dmf = A.alloc((256,), F32); bdmf = k.buf("dmf")
        mark = A.off
        QT = A.alloc((8, 512), BF16); bQT = k.buf("QT")
        ktw = [A.alloc((8, 256), BF16) for _ in range(2)]; bktw = [k.buf(f"ktw{i}") for i in range(2)]
        vw = [A.alloc((2, D), BF16) for _ in range(2)]; bvw = [k.buf(f"vw{i}") for i in range(2)]
        Sb = [A.alloc((4, 256), F32) for _ in range(2)]; bSb = [k.buf(f"Sb{i}") for i in range(2)]
        Pt = [A.alloc((4, 256), BF16) for _ in range(2)]; bP = [k.buf(f"P{i}") for i in range(2)]
        PT = [A.alloc((4, 2, 128), BF16) for _ in range(2)]; bPT = [k.buf(f"PT{i}") for i in range(2)]
        accrow = [A.alloc((ACCW,), F32) for _ in range(2)]; bacc = [k.buf(f"accrow{i}") for i in range(2)]
        negm = A.alloc((16,), F32); bnegm = k.buf("negm")
        prod = A.alloc((8, 128), BF16); bprod = k.buf("prod")
        ind2 = A.alloc((2,), BF16)
        k.ms(PL, ind2, 0.0, [bc])
        k.ms(PL, ind2[0:64, 0:1], 1.0, [bc])
        k.ms(PL, ind2[64:128, 1:2], 1.0, [bc])
        ktws = A.alloc((8, 256), BF16); bktws = k.buf("ktws")
        vws = A.alloc((2, D), BF16); bvws = k.buf("vws")
        Kc = A.alloc((D,), BF16); bKc = k.buf("Kc")
        QTs = A.alloc((8, 128), BF16); bQTs = k.buf("QTs")
        qsT = A.alloc((24, NSAMP), BF16); bqsT = k.buf("qsT")

        self.load_ln(layer)
        for kc in range(8):
            k.dma(PL, Wb[:, kc, :], self.b_in[j, kc * 128:(kc + 1) * 128, :], [bext], [bWb], bWb)
            k.dma(PL, Wo[:, kc, :], self.b_out[j, kc * 128:(kc + 1) * 128, :], [bext], [bWo], bWo)
        k.op(PL, lambda e: e.iota(dmf, pattern=[[1, 256]], base=-128, channel_multiplier=-1,
                                  allow_small_or_imprecise_dtypes=True), [], [bdmf])
        for g in range(3):
            k.ts(PL, DM[:, g, :], dmf, float(DILS[g]), None, ALU.mult, None, [bdmf], [bDM])
            k.op(PL, lambda e, g=g: e.affine_select(out=DM[:, g, :], in_=DM[:, g, :], pattern=[[-1, 256]], compare_op=ALU.is_ge,
                                                   fill=-32768.0, base=128, channel_multiplier=1), [bDM], [bDM])
            k.op(PL, lambda e, g=g: e.affine_select(out=DM[:, g, :], in_=DM[:, g, :], pattern=[[1, 256]], compare_op=ALU.is_ge,
                                                   fill=-32768.0, base=0, channel_multiplier=-1), [bDM], [bDM])
        k.ms(PL, ktws, 0.0, [bktws])
        k.ms(PL, vws, 0.0, [bvws])
        k.ms(PL, QTs, 0.0, [bQTs])
        for c in range(24):
            for kc in range(8):
                k.mm(PB[:, c * 4:(c + 1) * 4], Wb[:, kc, c * 128:(c + 1) * 128], self.hsT[:, kc, :], kc == 0, kc == 7,
                     [bWb, self.bhsT], [bPB[0]])
        k.actf(qsT, PB[:, 0:96].rearrange("p (c b) -> p c b", c=24), AF.Copy, [bPB[0]], [bqsT], scale=0.125)

        def attn_tile(g, qt3, bq, kt, bkt, vt, bvt, first_block, arow, barow):
            k0 = 128 if first_block else 0
            nk = 256 - k0
            nkb = nk // 128
            k.tt(V, prod, qt3, kt[:, :, 128:256], ALU.mult, [bq, bkt], [bprod])
            for fc in range(8):
                k.mm(PY[:, 2 * fc:2 * fc + 2], prod[:, fc, :], ind2[:, :], True, True, [bprod, bc], [bPY[0]])
            k.cp(V, arow[:, 1040:1056], PY[:, 0:16], [bPY[0]], [barow])
            k.ts(V, negm, PY[:, 0:16], -1.0, None, ALU.mult, None, [bPY[0]], [bnegm])

            def stage_qk(hg):
                i = hg % 2
                for hh in range(4):
                    h = hg * 4 + hh
                    fc, hp = h // 2, h % 2
                    bank = 2 * i + hh % 2
                    c0 = bank * 512 + (hh // 2) * 256
                    k.mm(PY[:, c0:c0 + nk], qt3[hp * 64:(hp + 1) * 64, fc, :], kt[hp * 64:(hp + 1) * 64, fc, k0:256], True, True,
                         [bq, bkt], [bPY[bank]])
                p4 = PY[:, 2 * i * 512:(2 * i + 2) * 512].rearrange("p (bank half kk) -> p bank half kk", bank=2, half=2)[:, :, :, 0:nk]
                s4 = Sb[i].rearrange("p (half bank) kk -> p bank half kk", bank=2)[:, :, :, 0:nk]
                a4 = AB[:, hg * 4:(hg + 1) * 4, :].rearrange("p (half bank) kk -> p bank half kk", bank=2)[:, :, :, k0:256]
                k.tt(V, s4, p4, a4, ALU.add, [bPY[2 * i], bPY[2 * i + 1], bAB], [bSb[i]])
                for hh in range(4):
                    h = hg * 4 + hh
                    k.actf(Pt[i][:, hh, 0:nk], Sb[i][:, hh, 0:nk], AF.Exp, [bSb[i], bnegm], [bP[i]], bias=negm[:, h:h + 1], scale=1.0)

            def stage_pv(hg):
                i = hg % 2
                ptp, bptp = PB[:, :].bitcast(BF16), bPB[0]
                for hh in range(4):
                    for kb in range(nkb):
                        c0 = (hh * nkb + kb) * 128
                        k.tr(ptp[:, c0:c0 + 128], Pt[i][:, hh, kb * 128:(kb + 1) * 128], idb[:, :], [bP[i], bc], [bptp])
                k.cp(V, PT[i][:, :, 0:nkb, :], ptp[:, 0:4 * nkb * 128].rearrange("p (a b c) -> p a b c", a=4, b=nkb), [bptp], [bPT[i]])
                for hh in range(4):
                    h = hg * 4 + hh
                    for kb in range(nkb):
                        k.mm(PA[:, h * 64:(h + 1) * 64], PT[i][:, hh, kb, :], vt[:, k0 // 128 + kb, h * 64:(h + 1) * 64],
                             kb == 0, kb == nkb - 1, [bPT[i], bvt], [bPA[h // 8]])
                        k.mm(PC[:, h:h + 1], PT[i][:, hh, kb, :], self.onesb[:, 0:1], kb == 0, kb == nkb - 1, [bPT[i], bc], [bPC[0]])

            stage_qk(0)
            for hg in range(4):
                if hg + 1 < 4:
                    stage_qk(hg + 1)
                stage_pv(hg)
            k.cp(ACT, arow[:, 0:512], PA[:, 0:512], [bPA[0]], [barow])
            k.cp(V, arow[:, 512:1024], PA[:, 512:1024], [bPA[1]], [barow])
            k.cp(V, arow[:, 1024:1040], PC[:, 0:16], [bPC[0]], [barow])

        def build_ab(g):
            for h in range(16):
                k.ts(V, AB[:, h, :], DM[:, g, :], float(2.0 ** (-(h + 1) / 2.0)), None, ALU.mult, None, [bDM], [bAB])

        it = 0
        for g in range(3):
            dil = DILS[g]
            n = SEQ // dil
            build_ab(g)
            for q4 in range(4):
                pb = q4 * 512
                for fc in range(8):
                    ps, bps = (PB[:, :], bPB[0]) if fc % 2 == 0 else (PC[:, :], bPC[0])
                    for kc in range(8):
                        k.mm(ps, Wb[:, kc, g * 1024 + fc * 128:g * 1024 + (fc + 1) * 128], self.tok_ap(g, kc, pb, 512),
                             kc == 0, kc == 7, [bWb] + allhT, [bps])
                    k.actf(QT[:, fc, :], ps, AF.Copy, [bps], [bQT], scale=0.125)
                for tt in range(4):
                    pos0 = pb + tt * 128
                    i0 = pos0 % n
                    first = (i0 == 0)
                    i2 = it % 2
                    it += 1
                    ktv = self.KTs[g].rearrange("(c p) t -> p c t", p=128)
                    if first:
                        k.dma(SP, ktw[i2][:, :, 128:256], ktv[:, :, pos0:pos0 + 128], [self.bKT[g]], [bktw[i2]], bktw[i2])
                        k.dma(SP, vw[i2][:, 1, :], self.Vs[g][pos0:pos0 + 128, :], [self.bVs[g]], [bvw[i2]], bvw[i2])
                    else:
                        k.dma(SP, ktw[i2], ktv[:, :, pos0 - 128:pos0 + 128], [self.bKT[g]], [bktw[i2]], bktw[i2])
                        k.dma(SP, vw[i2], self.Vs[g][pos0 - 128:pos0 + 128, :].rearrange("(a p) c -> p a c", p=128),
                              [self.bVs[g]], [bvw[i2]], bvw[i2])
                    attn_tile(g, QT[:, :, tt * 128:(tt + 1) * 128], bQT, ktw[i2], bktw[i2], vw[i2], bvw[i2],
                              first, accrow[i2], bacc[i2])
                    k.dma(PL, self.row_ap(self.ACC[g][0:SEQ, :], g, pos0), accrow[i2], [bacc[i2]], [self.bACC[g]], bacc[i2])
        for g in range(3):
            dil = DILS[g]
            build_ab(g)
            for b in range(NSAMP):
                i2 = it % 2
                it += 1
                cv = self.ck[g][b].rearrange("(i r) c -> r i c", r=dil)[0]
                k.dma(PL, Kc, cv[:, 0:1024], [bext], [bKc], bKc)
                for half in range(2):
                    for f4 in range(4):
                        fc = half * 4 + f4
                        k.tr(PYb[:, 2048 + half * 1024 + f4 * 128:2048 + half * 1024 + (f4 + 1) * 128], Kc[:, fc * 128:(fc + 1) * 128],
                             idb[:, :], [bKc, bc], [bPY[2 + half]])
                    k.cp(V if half else ACT, ktws[:, half * 4:(half + 1) * 4, 127:255],
                         PYb[:, 2048 + half * 1024:2048 + half * 1024 + 512].rearrange("p (a b) -> p a b", a=4), [bPY[2 + half]], [bktws])
                k.cp(V, ktws[:, :, 255:256], self.knewT[:, g * 8:(g + 1) * 8, b:b + 1], [self.bknew], [bktws])
                k.dma(PL, vws[127:128, 0, :], cv[0:1, 1024:2048], [bext], [bvws], bvws)
                k.dma(PL, vws[0:127, 1, :], cv[1:128, 1024:2048], [bext], [bvws], bvws)
                k.dma(PL, vws[127:128, 1, :], self.KVS[b:b + 1, 3072 + g * 1024:3072 + (g + 1) * 1024], [self.bKVS], [bvws], bvws)
                k.cp(V, QTs[:, :, 127:128], qsT[:, g * 8:(g + 1) * 8, b:b + 1], [bqsT], [bQTs])
                attn_tile(g, QTs, bQTs, ktws, bktws, vws, bvws, False, accrow[i2], bacc[i2])
                k.dma(PL, self.ACC[g][SEQ + b:SEQ + b + 1, :], accrow[i2][127:128, :], [bacc[i2]], [self.bACC[g]], bacc[i2])

        k.barrier()
        A.off = mark
        sets = []
        for i in range(2):
            sets.append((A.alloc((3, ACCW), F32), k.buf(f"acc3{i}"), A.alloc((16 * 12,), F32), k.buf(f"msm{i}"),
                         A.alloc((D,), F32), k.buf(f"sg{i}"), A.alloc((D,), BF16), k.buf(f"og{i}"),
                         A.alloc((8, 128), BF16), k.buf(f"ogT{i}"), A.alloc((D,), F32), k.buf(f"hin{i}"),
                         A.alloc((D,), F32), k.buf(f"r{i}"), A.alloc((8,), F32), k.buf(f"stat{i}"),
                         A.alloc((D,), BF16), k.buf(f"junk2{i}")))
        for c in range(NCHP + 1):
            (acc3, bacc3, msm, bmsm, sg, bsg, og, bog, ogT, bogT, hin, bhin, r, br, stat, bstat, junk, bjunk) = sets[c % 2]
            samp = (c == NCHP)
            np_ = NSAMP if samp else 128
            sl = slice(0, np_)
            for g in range(3):
                src = self.ACC[g][SEQ:SEQ + NSAMP, :] if samp else self.ACC[g][c * 128:(c + 1) * 128, :]
                k.dma(SP, acc3[sl, g, :], src, [self.bACC[g]], [bacc3], bacc3)
            if samp:
                k.dma(SP, hin[sl, :], self.S[SEQ:NCH * 128, :].rearrange("(b p) d -> p b d", p=128)[0], [self.bS[c] for c in range(NCHP, NCH)],
                      [bhin], bhin)
            else:
                k.dma(SP, hin[sl, :], self.S[c * 128:(c + 1) * 128, :], [self.bS[c]], [bhin], bhin)
            m3 = acc3[sl, :, 1040:1056]
            s3 = acc3[sl, :, 1024:1040]
            mmax = msm[sl, 0:16]
            w3 = msm[sl, 16:64].rearrange("p (g h) -> p g h", g=3)
            den = msm[sl, 64:80]
            rden = msm[sl, 80:96]
            co3 = msm[sl, 96:144].rearrange("p (g h) -> p g h", g=3)
            ws3 = msm[sl, 144:192].rearrange("p (g h) -> p g h", g=3)
            k.tt(V, mmax, acc3[sl, 0, 1040:1056], acc3[sl, 1, 1040:1056], ALU.max, [bacc3], [bmsm])
            k.tt(V, mmax, mmax, acc3[sl, 2, 1040:1056], ALU.max, [bacc3, bmsm], [bmsm])
            k.tt(V, w3, m3, mmax.unsqueeze(1).to_broadcast([np_, 3, 16]), ALU.subtract, [bacc3, bmsm], [bmsm])
            k.actf(w3, w3, AF.Exp, [bmsm], [bmsm])
            k.tt(V, ws3, w3, s3, ALU.mult, [bmsm, bacc3], [bmsm])
            k.tt(V, den, ws3[:, 0, :], ws3[:, 1, :], ALU.add, [bmsm], [bmsm])
            k.tt(V, den, den, ws3[:, 2, :], ALU.add, [bmsm], [bmsm])
            k.recip(rden, den, [bmsm], [bmsm])
            k.tt(V, co3, w3, rden.unsqueeze(1).to_broadcast([np_, 3, 16]), ALU.mult, [bmsm], [bmsm])
            for g in range(3):
                og3 = acc3[sl, g, 0:1024].rearrange("p (h d) -> p h d", h=16)
                k.tt(V, og3, og3, co3[:, g, :].unsqueeze(2).to_broadcast([np_, 16, 64]), ALU.mult, [bacc3, bmsm], [bacc3])
            k.tt(V, acc3[sl, 0, 0:1024], acc3[sl, 0, 0:1024], acc3[sl, 1, 0:1024], ALU.add, [bacc3], [bacc3])
            k.tt(V, acc3[sl, 0, 0:1024], acc3[sl, 0, 0:1024], acc3[sl, 2, 0:1024], ALU.add, [bacc3], [bacc3])
            for half in range(2):
                col = 3072 + half * 512
                for kc in range(8):
                    lhs = self.hsT[:, kc, :] if samp else self.hTall[:, kc, c * 128:(c + 1) * 128]
                    k.mm(PY[sl, half * 512:(half + 1) * 512], lhs, Wb[:, kc, col:col + 512], kc == 0, kc == 7,
                         [bWb, self.bhsT if samp else allhT[c]], [bPY[half]])
            k.actf(sg[sl, :], PY[sl, 0:1024], AF.Silu, bPY[0:2], [bsg])
            k.tt(V, og[sl, :], acc3[sl, 0, 0:1024], sg[sl, :], ALU.mult, [bacc3, bsg], [bog])
            for fc in range(8):
                k.tr(PAb[:, fc * 128:fc * 128 + np_], og[sl, fc * 128:(fc + 1) * 128], idb[sl, sl], [bog, bc], [bPA[0]])
            k.cp(ACT, ogT[:, :, 0:np_], PAb[:, 0:1024].rearrange("p (a b) -> p a b", a=8)[:, :, 0:np_], [bPA[0]], [bogT])
            for half in range(2):
                for fc in range(8):
                    k.mm(PY[sl, 1024 + half * 512:1024 + (half + 1) * 512], ogT[:, fc, 0:np_], Wo[:, fc, half * 512:(half + 1) * 512],
                         fc == 0, fc == 7, [bogT, bWo], [bPY[2 + half]])
            self.resid_ln(PY[sl, 1024:2048], bPY[2:4], hin[sl, :], bhin, r[sl, :], br, stat[sl, :], bstat, junk[sl, :], bjunk, np_=np_)
            if last:
                dst = self.ys if samp else self.yp[c * 128:(c + 1) * 128, :]
                k.dma(PL, dst, r[sl, :], [br], [bext], br)
            else:
                if samp:
                    k.dma(PL, self.S[SEQ:NCH * 128, :].rearrange("(b p) d -> p b d", p=128)[0], r[sl, :], [br],
                          [self.bS[c] for c in range(NCHP, NCH)], br)
                else:
                    k.dma(PL, self.S[c * 128:(c + 1) * 128, :], r[sl, :], [br], [self.bS[c]], br)
                for kc in range(8):
                    k.tr(PA[:, 512 + kc * 64:512 + kc * 64 + np_] if False else PA[:, kc * 128:kc * 128 + np_],
                         r[sl, kc * 128:(kc + 1) * 128], idf[sl, sl], [br, bc], [bPA[kc // 4]])
                if samp:
                    k.cp(ACT, self.hsT[:, :, :], PA[:, :].rearrange("p (a b) -> p a b", a=8)[:, :, 0:NSAMP], bPA, [self.bhsT])
                else:
                    k.cp(ACT, self.hTall[:, :, c * 128:(c + 1) * 128], PA[:, :].rearrange("p (a b) -> p a b", a=8), bPA, [allhT[c]])

_IN_NAMES = ["x_prompt", "x_sample", "state_ssm", "state_conv", "cache_kv_w128", "cache_kv_w512", "cache_kv_w2048",
             "a_in_proj", "a_conv_w", "a_conv_b", "a_dt_bias", "a_log", "a_d", "a_norm_w", "a_out_proj",
             "kv_proj", "b_in_proj", "b_out_proj", "ln_g", "ln_b"]


def make_in_maps(inp):
    f = lambda a: np.ascontiguousarray(np.asarray(a, dtype=np.float32))
    maps = []
    for i in range(NCORES):
        s = slice(NSAMP * i, NSAMP * (i + 1))
        m = {
            "xp": f(inp["x_prompt"][i]),
            "xs": f(inp["x_sample"][s, 0]),
            "st_ssm": f(np.asarray(inp["state_ssm"])[:, s].reshape(2, NSAMP, DI, 128)),
            "st_conv": f(np.asarray(inp["state_conv"])[:, s]),
            "ck128": f(np.asarray(inp["cache_kv_w128"])[s].reshape(NSAMP, 128, 2048)),
            "ck512": f(np.asarray(inp["cache_kv_w512"])[s].reshape(NSAMP, 512, 2048)),
            "ck2048": f(np.asarray(inp["cache_kv_w2048"])[s].reshape(NSAMP, 2048, 2048)),
            "a_in": f(inp["a_in_proj"]), "a_cw": f(inp["a_conv_w"]), "a_cb": f(inp["a_conv_b"]),
            "a_dtb": f(inp["a_dt_bias"]), "a_log": f(inp["a_log"]), "a_d": f(inp["a_d"]),
            "a_nw": f(inp["a_norm_w"]), "a_out": f(inp["a_out_proj"]), "kvw": f(inp["kv_proj"]),
            "b_in": f(inp["b_in_proj"]), "b_out": f(inp["b_out_proj"]), "ln_g": f(inp["ln_g"]), "ln_b": f(inp["ln_b"]),
        }
        maps.append(m)
    return maps


def kernel(**inputs):
    prog = Prog()
    maps = make_in_maps(inputs)
    res = run_bass_kernel_spmd(prog.nc, maps, core_ids=list(range(NCORES)))
    R = res.results
    cat = lambda name: np.concatenate([np.asarray(r[name]) for r in R], axis=0)
    y_prompt = np.stack([R[i]["yp"] for i in range(NCORES)]).reshape(8, SEQ, D)
    y_sample = cat("ys").reshape(32, 1, D)
    ssm_prompt = np.stack([R[i]["ssm_p"] for i in range(NCORES)], axis=1).reshape(2, 8, 32, 64, 128)
    conv_prompt = np.stack([R[i]["conv_p"] for i in range(NCORES)], axis=1).reshape(2, 8, 3, CD)
    kvp = [np.stack([R[i][n] for i in range(NCORES)]).reshape(8, w, 2, 16, 64)
           for n, w in (("kv128_p", 128), ("kv512_p", 512), ("kv2048_p", 2048))]
    ssm_sample = np.concatenate([R[i]["ssm_s"] for i in range(NCORES)], axis=1).reshape(2, 32, 32, 64, 128)
    conv_sample = np.concatenate([R[i]["conv_s"] for i in range(NCORES)], axis=1).reshape(2, 32, 3, CD)
    kvs = [cat(n).reshape(32, w, 2, 16, 64) for n, w in (("kv128_s", 128), ("kv512_s", 512), ("kv2048_s", 2048))]
    outs = (y_prompt, y_sample, ssm_prompt, conv_prompt, kvp[0], kvp[1], kvp[2],
            ssm_sample, conv_sample, kvs[0], kvs[1], kvs[2])
    return tuple(np.ascontiguousarray(o, dtype=np.float32) for o in outs)
```
